# Optimizing a Trainium2 kernel written in Bass

```python
import math
import jax, jax.numpy as jnp
from jax import lax
import numpy as np

D_MODEL = 2048
BATCH = 2
SEQ = 8192
DEPTH = 4

CHUNK = 64
N_MEM = 256
N_BRANCH = 3
D_CONV = D_MODEL // 2
CONV_WIDTH = 31
ATTN_HEAD_DIM = 64
D_ATTN = D_MODEL // 2
N_ATTN_HEADS = D_ATTN // ATTN_HEAD_DIM
LEFT_CHUNKS = 8
BAND = LEFT_CHUNKS + 1
MAX_REL = 256
N_MEM_HEADS = 4
D_MEM = D_MODEL // 2
MEM_HEAD_DIM = D_MEM // N_MEM_HEADS
D_FF = ((8 * D_MODEL // 3 + 255) // 256) * 256
FFN_CONV_WIDTH = 3
D_IN = 2 * D_CONV + 3 * D_ATTN + D_MEM + N_BRANCH * D_MODEL
SPLITS = (2 * D_CONV, 2 * D_CONV + 3 * D_ATTN, 2 * D_CONV + 3 * D_ATTN + D_MEM)
EPS = 1e-6
NEG_INF = -1e30

kernel_name = 'hybrid_gated_conv_chunkattn_mem_encoder'


def rmsnorm(x, g):
    xf = x.astype(jnp.float32)
    y = xf * lax.rsqrt(jnp.mean(xf * xf, axis=-1, keepdims=True) + EPS)
    return (y * g.astype(jnp.float32)).astype(x.dtype)


def layernorm(x, g, b):
    xf = x.astype(jnp.float32)
    mu = jnp.mean(xf, axis=-1, keepdims=True)
    var = jnp.mean(jnp.square(xf - mu), axis=-1, keepdims=True)
    y = (xf - mu) * lax.rsqrt(var + EPS)
    return (y * g.astype(jnp.float32) + b.astype(jnp.float32)).astype(x.dtype)


def causal_dwconv(x, w, b):
    width = w.shape[0]
    ch = x.shape[-1]
    y = lax.conv_general_dilated(
        x, w[:, None, :].astype(x.dtype), window_strides=(1,),
        padding=[(width - 1, 0)], dimension_numbers=('NWC', 'WIO', 'NWC'),
        feature_group_count=ch)
    return y + b.astype(x.dtype)


def conv_module(u, conv_w, conv_b, ln_g, ln_b):
    a, gt = jnp.split(u, 2, axis=-1)
    h = a * jax.nn.sigmoid(gt)
    h = causal_dwconv(h, conv_w, conv_b)
    return jax.nn.silu(layernorm(h, ln_g, ln_b))


def chunk_attention(q, k, v, rel_bias):
    b, s, h, hd = q.shape
    nc = s // CHUNK
    qc = q.reshape(b, nc, CHUNK, h, hd)
    pad = ((0, 0), (LEFT_CHUNKS, 0), (0, 0), (0, 0), (0, 0))
    kp = jnp.pad(k.reshape(b, nc, CHUNK, h, hd), pad)
    vp = jnp.pad(v.reshape(b, nc, CHUNK, h, hd), pad)
    band_idx = jnp.arange(nc)[:, None] + jnp.arange(BAND)[None, :]
    kb = kp[:, band_idx].reshape(b, nc, BAND * CHUNK, h, hd)
    vb = vp[:, band_idx].reshape(b, nc, BAND * CHUNK, h, hd)
    scores = jnp.einsum('bcqhd,bckhd->bhcqk', qc, kb).astype(jnp.float32) * (hd ** -0.5)
    qi = jnp.arange(CHUNK)[:, None]
    km = jnp.arange(BAND * CHUNK)[None, :]
    dist = LEFT_CHUNKS * CHUNK + qi - km
    rel_idx = jnp.clip(dist, -MAX_REL, MAX_REL) + MAX_REL
    bias = rel_bias.astype(jnp.float32)[:, rel_idx]
    valid = jnp.repeat(band_idx >= LEFT_CHUNKS, CHUNK, axis=1)
    scores = scores + bias[None, :, None, :, :]
    scores = jnp.where(valid[None, None, :, None, :], scores, NEG_INF)
    p = jax.nn.softmax(scores, axis=-1).astype(v.dtype)
    out = jnp.einsum('bhcqk,bckhd->bcqhd', p, vb)
    return out.reshape(b, s, h * hd)


def memory_attention(q, km, vm):
    b, s, h, hd = q.shape
    scores = jnp.einsum('bshd,bmhd->bhsm', q, km).astype(jnp.float32) * (hd ** -0.5)
    p = jax.nn.softmax(scores, axis=-1).astype(vm.dtype)
    out = jnp.einsum('bhsm,bmhd->bshd', p, vm)
    return out.reshape(b, s, h * hd)


def setup_inputs(seed: int = 0) -> dict:
    key = jax.random.key(seed)
    ks = jax.random.split(key, 24)

    def nrm(k, shape, scale):
        return jax.random.normal(k, shape, jnp.float32) * scale

    def gain(k, shape):
        return 1.0 + 0.05 * jax.random.normal(k, shape, jnp.float32)

    L = DEPTH
    return {
        'x': nrm(ks[0], (BATCH, SEQ, D_MODEL), 1.0),
        'mem': nrm(ks[1], (BATCH, N_MEM, D_MODEL), 1.0),
        'mix_norm_g': gain(ks[2], (L, D_MODEL)),
        'mem_norm_g': gain(ks[3], (L, D_MODEL)),
        'w_in': nrm(ks[4], (L, D_MODEL, D_IN), D_MODEL ** -0.5),
        'gate_b': nrm(ks[5], (L, N_BRANCH * D_MODEL), 0.1),
        'conv_w': nrm(ks[6], (L, CONV_WIDTH, D_CONV), CONV_WIDTH ** -0.5),
        'conv_b': nrm(ks[7], (L, D_CONV), 0.02),
        'conv_ln_g': gain(ks[8], (L, D_CONV)),
        'conv_ln_b': nrm(ks[9], (L, D_CONV), 0.02),
        'w_conv_out': nrm(ks[10], (L, D_CONV, D_MODEL), D_CONV ** -0.5),
        'rel_bias': nrm(ks[11], (L, N_ATTN_HEADS, 2 * MAX_REL + 1), 0.1),
        'w_attn_out': nrm(ks[12], (L, D_ATTN, D_MODEL), D_ATTN ** -0.5),
        'w_mem_kv': nrm(ks[13], (L, D_MODEL, 2 * D_MEM), D_MODEL ** -0.5),
        'w_mem_out': nrm(ks[14], (L, D_MEM, D_MODEL), D_MEM ** -0.5),
        'w_o': nrm(ks[15], (L, D_MODEL, D_MODEL), D_MODEL ** -0.5),
        'ffn_norm_g': gain(ks[16], (L, D_MODEL)),
        'w_up': nrm(ks[17], (L, D_MODEL, 2 * D_FF), D_MODEL ** -0.5),
        'ffn_conv_w': nrm(ks[18], (L, FFN_CONV_WIDTH, 2 * D_FF), FFN_CONV_WIDTH ** -0.5),
        'ffn_conv_b': nrm(ks[19], (L, 2 * D_FF), 0.02),
        'w_down': nrm(ks[20], (L, D_FF, D_MODEL), D_FF ** -0.5),
        'final_norm_g': gain(ks[21], (D_MODEL,)),
    }


def reference(x, mem, mix_norm_g, mem_norm_g, w_in, gate_b, conv_w, conv_b, conv_ln_g, conv_ln_b,
              w_conv_out, rel_bias, w_attn_out, w_mem_kv, w_mem_out, w_o, ffn_norm_g, w_up,
              ffn_conv_w, ffn_conv_b, w_down, final_norm_g):
    b, s, d = x.shape
    h = x
    for l in range(DEPTH):
        xn = rmsnorm(h, mix_norm_g[l])
        proj = xn @ w_in[l]
        u_conv, qkv, q_mem, gates = jnp.split(proj, SPLITS, axis=-1)
        y_conv = conv_module(u_conv, conv_w[l], conv_b[l], conv_ln_g[l], conv_ln_b[l]) @ w_conv_out[l]
        q, k, v = jnp.split(qkv.reshape(b, s, 3, N_ATTN_HEADS, ATTN_HEAD_DIM), 3, axis=2)
        y_attn = chunk_attention(q[:, :, 0], k[:, :, 0], v[:, :, 0], rel_bias[l]) @ w_attn_out[l]
        mn = rmsnorm(mem, mem_norm_g[l])
        kv = (mn @ w_mem_kv[l]).reshape(b, mem.shape[1], 2, N_MEM_HEADS, MEM_HEAD_DIM)
        qm = q_mem.reshape(b, s, N_MEM_HEADS, MEM_HEAD_DIM)
        y_mem = memory_attention(qm, kv[:, :, 0], kv[:, :, 1]) @ w_mem_out[l]
        g = jax.nn.sigmoid(gates + gate_b[l]).reshape(b, s, N_BRANCH, d)
        merged = g[:, :, 0] * y_conv + g[:, :, 1] * y_attn + g[:, :, 2] * y_mem
        h = h + merged @ w_o[l]
        hn = rmsnorm(h, ffn_norm_g[l])
        up = causal_dwconv(hn @ w_up[l], ffn_conv_w[l], ffn_conv_b[l])
        val, gt = jnp.split(up, 2, axis=-1)
        h = h + (jax.nn.silu(gt) * val) @ w_down[l]
    return rmsnorm(h, final_norm_g)
```

```python
from contextlib import ExitStack
import numpy as np
import concourse.bass as bass
import concourse.mybir as mybir
from concourse.bass_utils import run_bass_kernel_spmd

F32 = mybir.dt.float32
BF16 = mybir.dt.bfloat16
AF = mybir.ActivationFunctionType
ALU = mybir.AluOpType

D = 2048
KC = 16
T = 2048
HALO = 512
TH = T + HALO
DEPTH = 4
NCORES = 8
D_FF = 5632
NFC = 44
EPS = 1e-6
NEG = -30000.0


class Buf:
    __slots__ = ("last_w", "readers")

    def __init__(self):
        self.last_w = None
        self.readers = {}


class Eng:
    def __init__(self, nc, name, handle):
        self.name = name
        self.h = handle
        self.sem = nc.alloc_semaphore("s_" + name)
        self.count = 0
        self.seen = {}


class Trk:
    def __init__(self, nc, n_dma_sems=32):
        self.nc = nc
        self.pe = Eng(nc, "pe", nc.tensor)
        self.act = Eng(nc, "act", nc.scalar)
        self.dve = Eng(nc, "dve", nc.vector)
        self.pool = Eng(nc, "pool", nc.gpsimd)
        self.sp = Eng(nc, "sp", nc.sync)
        self.engs = [self.pe, self.act, self.dve, self.pool, self.sp]
        self.dsems = [[nc.alloc_semaphore(f"d{i}"), 0] for i in range(n_dma_sems)]
        self.dnext = 0

    def _wait(self, eng, toks):
        best = {}
        for (sem, val, key) in toks:
            if key not in best or best[key][1] < val:
                best[key] = (sem, val)
        for key, (sem, val) in best.items():
            if eng.seen.get(key, 0) >= val:
                continue
            eng.h.wait_ge(sem, val)
            eng.seen[key] = val

    def _deps(self, eng, reads, writes):
        toks = []
        for b in reads:
            if b.last_w is not None:
                if not (b.last_w[2] == eng.name and eng.name == "pe"):
                    toks.append(b.last_w)
        for b in writes:
            if b.last_w is not None and b.last_w[2] != eng.name:
                toks.append(b.last_w)
            for k, tk in b.readers.items():
                if k != eng.name:
                    toks.append(tk)
        return toks

    def op(self, eng, fn, reads=(), writes=()):
        self._wait(eng, self._deps(eng, reads, writes))
        ins = fn()
        eng.count += 1
        ins.then_inc(eng.sem, 1)
        tok = (eng.sem, eng.count, eng.name)
        for b in reads:
            b.readers[eng.name] = tok
        for b in writes:
            b.last_w = tok
            b.readers = {}
        return ins

    def dma(self, eng, out, in_, reads=(), writes=(), **kw):
        toks = self._deps(eng, reads, writes)
        slot = self.dsems[self.dnext]
        key = f"dma{self.dnext}"
        self.dnext = (self.dnext + 1) % len(self.dsems)
        if slot[1] > 0:
            toks.append((slot[0], slot[1], key))
        self._wait(eng, toks)
        ins = eng.h.dma_start(out=out, in_=in_, **kw)
        slot[1] += 16
        ins.then_inc(slot[0], 16)
        tok = (slot[0], slot[1], key)
        for b in reads:
            b.readers[key] = tok
        for b in writes:
            b.last_w = tok
            b.readers = {}
        return ins

    def barrier(self):
        toks = [(e.sem, e.count, e.name) for e in self.engs if e.count > 0]
        toks += [(s[0], s[1], f"dma{i}") for i, s in enumerate(self.dsems) if s[1] > 0]
        for e in self.engs:
            self._wait(e, [tk for tk in toks if tk[2] != e.name])


_uid = [0]


def uid():
    _uid[0] += 1
    return _uid[0]


class Ring:
    def __init__(self, es, nc, name, shape, dtype, n):
        self.t = [es.enter_context(nc.sbuf_tensor(f"{name}_{i}_{uid()}", shape, dtype)) for i in range(n)]
        self.b = [Buf() for _ in range(n)]
        self.i = 0

    def next(self):
        k = self.i % len(self.t)
        self.i += 1
        return self.t[k], self.b[k]


class Ctx:
    def __init__(self, nc):
        self.nc = nc
        self.t = Trk(nc)
        self.psall = nc.alloc_psum_tensor("psall", [128, 8, 512], F32)
        self.psb = [Buf() for _ in range(8)]
        self.psi = 0
        self.ev = 0

    def ps(self):
        k = self.psi % 8
        self.psi += 1
        return self.psall[:, k, :], self.psb[k]

    def ps_pair(self):
        if self.psi % 2:
            self.psi += 1
        k = self.psi % 8
        self.psi += 2
        flat = self.psall[:, k:k + 2, :].rearrange("p a b -> p (a b)")
        return flat, [self.psb[k], self.psb[k + 1]]


def mm_group(nc, out, pairs):
    n = len(pairs)
    ins = None
    for i, (l, r) in enumerate(pairs):
        ins = nc.tensor.matmul(out, l, r, start=(i == 0), stop=(i == n - 1))
    return ins


def sb(es, nc, name, shape, dtype):
    return es.enter_context(nc.sbuf_tensor(f"{name}_{uid()}", shape, dtype))


def rmsnorm_phase(cx, es, srcT, tiles, gvec, xn, xnb, eps_t, ones_bf):
    nc, t = cx.nc, cx.t
    src = srcT.rearrange("(kc p) t -> p kc t", p=128)
    hring = Ring(es, nc, "nh", [128, KC, 512], F32, 2)
    sqring = Ring(es, nc, "nsq", [128, KC, 512], BF16, 2)
    sdring = Ring(es, nc, "nsd", [128, 512], F32, 2)
    rsring = Ring(es, nc, "nrs", [128, 512], F32, 2)
    for ti, (j0, n) in enumerate(tiles):
        hb, hbb = hring.next()
        t.dma(t.sp, hb[:, :, :n], src[:, :, j0:j0 + n], writes=[hbb])
        sq, sqb = sqring.next()
        t.op(t.act, lambda: nc.scalar.activation(out=sq[:, :, :n], in_=hb[:, :, :n], func=AF.Square),
             reads=[hbb], writes=[sqb])
        ps, psb = cx.ps()
        t.op(t.pe, lambda: mm_group(nc, ps[:, :n], [(ones_bf[:, :], sq[:, kc, :n]) for kc in range(KC)]),
             reads=[sqb], writes=[psb])
        sd, sdb = sdring.next()
        t.op(t.act, lambda: nc.scalar.activation(out=sd[:, :n], in_=ps[:, :n], func=AF.Sqrt,
                                                 bias=eps_t[:, 0:1], scale=1.0 / D),
             reads=[psb], writes=[sdb])
        rs, rsb = rsring.next()
        t.op(t.dve, lambda: nc.vector.reciprocal(rs[:, :n], sd[:, :n]), reads=[sdb], writes=[rsb])

        def body():
            ins = None
            for kc in range(KC):
                ins = nc.vector.scalar_tensor_tensor(out=xn[:, kc, j0:j0 + n], in0=hb[:, kc, :n],
                                                     scalar=gvec[:, kc:kc + 1], in1=rs[:, :n],
                                                     op0=ALU.mult, op1=ALU.mult)
            return ins
        t.op(t.dve, body, reads=[hbb, rsb], writes=[xnb[ti]])


def load_consts(cx, es, ident_d):
    nc, t = cx.nc, cx.t
    c = {}
    c["eps"] = sb(es, nc, "eps", [128, 1], F32)
    c["ones_bf"] = sb(es, nc, "ones_bf", [128, 128], BF16)
    c["b"] = Buf()
    t.op(t.dve, lambda: nc.vector.memset(c["eps"][:], EPS), writes=[c["b"]])
    t.op(t.dve, lambda: nc.vector.memset(c["ones_bf"][:], 1.0), writes=[c["b"]])
    return c


V_MIXG = 0
V_GATEB = 16
V_CONVB = 64
V_LNG = 72
V_LNB = 80
V_CONVW = 88
V_MEMG = 88 + 248
NV_MIX = 352


def build_mixer(dbg=False, stop=99):
    nc = bass.Bass("TRN2", target_bir_lowering=False)
    cx = Ctx(nc)
    t = cx.t

    def din(name, shape):
        return nc.dram_tensor(name, shape, F32, kind="ExternalInput").ap()

    def dscr(name, shape, dt):
        if dbg:
            return nc.dram_tensor(name, shape, dt, kind="ExternalOutput").ap()
        return nc.dram_tensor(name, shape, dt).ap()

    hT = din("hT", [D, TH])
    memT = din("memT", [D, 256])
    vecs_d = din("vecs", [128, NV_MIX])
    w_in = din("w_in", [D, 12288])
    w_conv_out = din("w_conv_out", [1024, D])
    w_attn_out = din("w_attn_out", [1024, D])
    w_mem_kv = din("w_mem_kv", [D, 2048])
    w_mem_out = din("w_mem_out", [1024, D])
    w_o = din("w_o", [D, D])
    biasblk = din("biasblk", [16, 128, 640])
    halo_ones_d = din("halo_ones", [128, 64])
    ident_d = din("ident", [128, 128])
    hmid = nc.dram_tensor("hmid", [D, T], F32, kind="ExternalOutput").ap()

    qT_d = dscr("qT_d", [1024, T], BF16)
    kT_d = dscr("kT_d", [1024, TH], BF16)
    V_d = dscr("V_d", [TH, 1024], BF16)
    qm_d = dscr("qm_d", [1024, T], BF16)
    gates_d = dscr("gates_d", [6144, T], BF16)
    hglu_d = dscr("hglu_d", [1024, 2080], BF16)
    cact_d = dscr("cact_d", [1024, T], BF16)
    attn_d = dscr("attn_d", [1024, T], BF16)
    memo_d = dscr("memo_d", [1024, T], BF16)
    merged_d = dscr("merged_d", [D, T], BF16)

    win_v = w_in.rearrange("(kc p) n -> p kc n", p=128)

    with ExitStack() as g:
        vecs = sb(g, nc, "vecs", [128, NV_MIX], F32)
        vb = Buf()
        t.dma(t.sp, vecs[:], vecs_d, writes=[vb])
        cn = load_consts(cx, g, ident_d)
        eps_t, ones_bf = cn["eps"], cn["ones_bf"]
        ident_bf = sb(g, nc, "ident_bf", [128, 128], BF16)
        ident_f = sb(g, nc, "ident_f", [128, 128], F32)
        t.dma(t.pool, ident_bf[:], ident_d, writes=[cn["b"]])
        t.dma(t.sp, ident_f[:], ident_d, writes=[cn["b"]])
        t.barrier()
        if stop <= 0:
            return nc

        with ExitStack() as es:
            xn = sb(es, nc, "xn", [128, KC, TH], BF16)
            ntile = [(i * 512, 512) for i in range(5)]
            xnb = [Buf() for _ in ntile]
            with ExitStack() as es1:
                rmsnorm_phase(cx, es1, hT, ntile, vecs[:, V_MIXG:V_MIXG + 16], xn, xnb, eps_t, ones_bf)
                t.barrier()
                if stop <= 1:
                    return nc

            def xbufs(j0, n):
                return [xnb[i] for i in range(5) if i * 512 < j0 + n and (i + 1) * 512 > j0]

            wring = Ring(es, nc, "w", [128, KC, 512], BF16, 3)
            stq = Ring(es, nc, "stq", [128, TH], BF16, 2)
            stv = Ring(es, nc, "stv", [128, 512], BF16, 3)
            sgr = Ring(es, nc, "sg", [128, 416], F32, 2)

            groups = [("glu", i) for i in range(4)] + [("q", i) for i in range(2)] + \
                     [("k", i) for i in range(2)] + [("v", i) for i in range(2)] + \
                     [("qm", i) for i in range(2)] + [("gate", i) for i in range(12)]
            loaded = {}

            def load(gi):
                kind, i = groups[gi]
                w, wb = wring.next()
                if kind == "glu":
                    t.dma(t.pool, w[:, :, 0:256], win_v[:, :, i * 256:(i + 1) * 256], writes=[wb])
                    t.dma(t.pool, w[:, :, 256:512], win_v[:, :, 1024 + i * 256:1024 + (i + 1) * 256], writes=[wb])
                else:
                    base = {"q": 2048, "k": 3072, "v": 4096, "qm": 5120, "gate": 6144}[kind]
                    t.dma(t.pool, w[:], win_v[:, :, base + i * 512: base + (i + 1) * 512], writes=[wb])
                loaded[gi] = (w, wb)

            def evac_copy(dst, ps, psb, dstb, scale=None):
                cx.ev += 1
                if cx.ev % 2 == 0:
                    if scale is None:
                        t.op(t.act, lambda: nc.scalar.activation(out=dst, in_=ps, func=AF.Copy),
                             reads=[psb], writes=[dstb])
                    else:
                        t.op(t.act, lambda: nc.scalar.activation(out=dst, in_=ps, func=AF.Copy, scale=scale),
                             reads=[psb], writes=[dstb])
                else:
                    if scale is None:
                        t.op(t.dve, lambda: nc.vector.tensor_copy(dst, ps), reads=[psb], writes=[dstb])
                    else:
                        t.op(t.dve, lambda: nc.vector.tensor_scalar(out=dst, in0=ps, scalar1=scale, scalar2=None,
                                                                    op0=ALU.mult),
                             reads=[psb], writes=[dstb])

            load(0)
            load(1)
            for gi, (kind, i) in enumerate(groups):
                if gi + 2 < len(groups):
                    load(gi + 2)
                w, wb = loaded.pop(gi)
                if kind in ("q", "k", "qm", "gate"):
                    j_lo = 0 if kind == "k" else HALO
                    ntt = 5 if kind == "k" else 4
                    for mi in range(4):
                        m = i * 4 + mi
                        st, stb = stq.next()
                        for tt in range(ntt):
                            j0 = j_lo + tt * 512
                            ps, psb = cx.ps()
                            t.op(t.pe, lambda: mm_group(nc, ps, [(w[:, kc, mi * 128:(mi + 1) * 128],
                                                                  xn[:, kc, j0:j0 + 512]) for kc in range(KC)]),
                                 reads=[wb] + xbufs(j0, 512), writes=[psb])
                            dst = st[:, tt * 512:(tt + 1) * 512]
                            if kind == "gate":
                                t.op(t.act, lambda: nc.scalar.activation(
                                    out=dst, in_=ps, func=AF.Sigmoid,
                                    bias=vecs[:, V_GATEB + m:V_GATEB + m + 1], scale=1.0),
                                    reads=[psb, vb], writes=[stb])
                            elif kind == "q":
                                evac_copy(dst, ps, psb, stb, scale=0.125)
                            else:
                                evac_copy(dst, ps, psb, stb)
                        dd = {"q": qT_d, "k": kT_d, "qm": qm_d, "gate": gates_d}[kind]
                        t.dma(t.sp, dd[m * 128:(m + 1) * 128, :], st[:, :ntt * 512], reads=[stb])
                elif kind == "v":
                    for tk in range(20):
                        ps, psb = cx.ps()
                        t.op(t.pe, lambda: mm_group(nc, ps, [(xn[:, kc, tk * 128:(tk + 1) * 128], w[:, kc, :])
                                                              for kc in range(KC)]),
                             reads=[wb] + xbufs(tk * 128, 128), writes=[psb])
                        st, stb = stv.next()
                        evac_copy(st[:], ps, psb, stb)
                        t.dma(t.sp, V_d[tk * 128:(tk + 1) * 128, i * 512:(i + 1) * 512], st[:], reads=[stb])
                else:
                    for ci in range(2):
                        c = 2 * i + ci
                        st, stb = stq.next()
                        for tt in range(5):
                            j0 = 480 + tt * 416
                            psa, psab = cx.ps()
                            psg, psgb = cx.ps()
                            t.op(t.pe, lambda: mm_group(nc, psa[:, :416], [(w[:, kc, ci * 128:(ci + 1) * 128],
                                                                            xn[:, kc, j0:j0 + 416]) for kc in range(KC)]),
                                 reads=[wb] + xbufs(j0, 416), writes=[psab])
                            t.op(t.pe, lambda: mm_group(nc, psg[:, :416], [(w[:, kc, 256 + ci * 128:256 + (ci + 1) * 128],
                                                                            xn[:, kc, j0:j0 + 416]) for kc in range(KC)]),
                                 reads=[wb] + xbufs(j0, 416), writes=[psgb])
                            sg, sgb = sgr.next()
                            t.op(t.act, lambda: nc.scalar.activation(out=sg[:], in_=psg[:, :416], func=AF.Sigmoid),
                                 reads=[psgb], writes=[sgb])
                            t.op(t.dve, lambda: nc.vector.tensor_tensor(out=st[:, tt * 416:(tt + 1) * 416],
                                                                        in0=psa[:, :416], in1=sg[:], op=ALU.mult),
                                 reads=[psab, sgb], writes=[stb])
                        t.dma(t.sp, hglu_d[c * 128:(c + 1) * 128, :], st[:, :2080], reads=[stb])
            t.barrier()

        if stop <= 2:
            return nc
        with ExitStack() as es:
            hg = sb(es, nc, "hg", [128, 8, 2080], BF16)
            hgb = [Buf() for _ in range(8)]
            hv = hglu_d.rearrange("(c p) t -> p c t", p=128)
            for c in range(8):
                t.dma(t.sp, hg[:, c, :], hv[:, c, :], writes=[hgb[c]])
            dg = sb(es, nc, "dg", [128, 8, 31, 128], BF16)
            dgb = [Buf() for _ in range(8)]
            for c in range(8):
                def mk():
                    ins = None
                    for k in range(31):
                        ins = nc.vector.tensor_scalar(out=dg[:, c, k, :], in0=ident_f[:, :],
                                                      scalar1=vecs[:, V_CONVW + c * 31 + k:V_CONVW + c * 31 + k + 1],
                                                      scalar2=None, op0=ALU.mult)
                    return ins
                t.op(t.dve, mk, reads=[vb], writes=[dgb[c]])
            onesS = sb(es, nc, "onesS", [128, 128], BF16)
            osb = Buf()
            t.op(t.dve, lambda: nc.vector.memset(onesS[:], 1.0 / 1024.0), writes=[osb])
            yr = Ring(es, nc, "y", [128, 8, 512], F32, 2)
            ybr = Ring(es, nc, "yb", [128, 8, 512], BF16, 2)
            ysr = Ring(es, nc, "ys", [128, 8, 512], BF16, 2)
            smr = Ring(es, nc, "sm", [128, 512], F32, 2)
            sqr_ = Ring(es, nc, "sq", [128, 512], F32, 2)
            srr = Ring(es, nc, "sr", [128, 512], F32, 2)
            tmr = Ring(es, nc, "tm", [128, 512], F32, 3)
            cst = Ring(es, nc, "cst", [128, 512], BF16, 3)
            for tt in range(4):
                y, yb = yr.next()
                ybf, ybfb = ybr.next()
                ysq, ysqb = ysr.next()
                for c in range(8):
                    ps, psb = cx.ps()
                    t.op(t.pe, lambda: mm_group(nc, ps, [(dg[:, c, k, :], hg[:, c, tt * 512 + 2 + k: tt * 512 + 2 + k + 512])
                                                          for k in range(31)]),
                         reads=[dgb[c], hgb[c]], writes=[psb])
                    cb = vecs[:, V_CONVB + c:V_CONVB + c + 1]
                    t.op(t.act, lambda: nc.scalar.activation(out=y[:, c, :], in_=ps, func=AF.Identity, bias=cb, scale=1.0),
                         reads=[psb, vb], writes=[yb])
                    t.op(t.dve, lambda: nc.vector.tensor_copy(ybf[:, c, :], y[:, c, :]),
                         reads=[yb], writes=[ybfb])
                    t.op(t.act, lambda: nc.scalar.activation(out=ysq[:, c, :], in_=ps, func=AF.Square, bias=cb, scale=1.0),
                         reads=[psb, vb], writes=[ysqb])
                pm, pmb = cx.ps()
                t.op(t.pe, lambda: mm_group(nc, pm, [(onesS[:, :], ybf[:, c, :]) for c in range(8)]),
                     reads=[osb, ybfb], writes=[pmb])
                pe2, pe2b = cx.ps()
                t.op(t.pe, lambda: mm_group(nc, pe2, [(onesS[:, :], ysq[:, c, :]) for c in range(8)]),
                     reads=[osb, ysqb], writes=[pe2b])
                sm, smb = smr.next()
                sq, sqb = sqr_.next()
                sr, srb = srr.next()
                t.op(t.act, lambda: nc.scalar.activation(out=sm[:], in_=pm, func=AF.Copy), reads=[pmb], writes=[smb])
                t.op(t.act, lambda: nc.scalar.activation(out=sq[:], in_=pm, func=AF.Square), reads=[pmb], writes=[sqb])
                t.op(t.dve, lambda: nc.vector.tensor_tensor(out=sq[:], in0=pe2, in1=sq[:], op=ALU.subtract),
                     reads=[pe2b, sqb], writes=[sqb])
                t.op(t.act, lambda: nc.scalar.activation(out=sq[:], in_=sq[:], func=AF.Sqrt, bias=eps_t[:, 0:1], scale=1.0),
                     reads=[sqb], writes=[sqb])
                t.op(t.dve, lambda: nc.vector.reciprocal(sr[:], sq[:]), reads=[sqb], writes=[srb])
                for c in range(8):
                    tm, tmb = tmr.next()
                    t.op(t.dve, lambda: nc.vector.tensor_tensor(out=tm[:], in0=y[:, c, :], in1=sm[:], op=ALU.subtract),
                         reads=[yb, smb], writes=[tmb])
                    t.op(t.dve, lambda: nc.vector.tensor_tensor(out=tm[:], in0=tm[:], in1=sr[:], op=ALU.mult),
                         reads=[tmb, srb], writes=[tmb])
                    cs, csb = cst.next()
                    t.op(t.act, lambda: nc.scalar.activation(out=cs[:], in_=tm[:], func=AF.Silu,
                                                             bias=vecs[:, V_LNB + c:V_LNB + c + 1],
                                                             scale=vecs[:, V_LNG + c:V_LNG + c + 1]),
                         reads=[tmb, vb], writes=[csb])
                    t.dma(t.sp, cact_d[c * 128:(c + 1) * 128, tt * 512:(tt + 1) * 512], cs[:], reads=[csb])
            t.barrier()

        if stop <= 3:
            return nc
        with ExitStack() as es:
            ho = sb(es, nc, "halo_ones", [128, 64], BF16)
            hob = Buf()
            t.dma(t.pool, ho[:], halo_ones_d, writes=[hob])
            kv_ = kT_d.rearrange("(c p) t -> p c t", p=128)
            qv_ = qT_d.rearrange("(c p) t -> p c t", p=128)
            vv_ = V_d.rearrange("(k p) c -> p k c", p=128)
            bv_ = biasblk.rearrange("h p c -> p h c")
            ptr = Ring(es, nc, "pt", [128, 640], BF16, 3)
            rcr = Ring(es, nc, "rc", [128, 128], F32, 3)
            aor = Ring(es, nc, "ao", [128, T], BF16, 2)
            for hh in range(2):
                with ExitStack() as es2:
                    kT = sb(es2, nc, "kT", [128, 4, TH], BF16)
                    qT = sb(es2, nc, "qT", [128, 4, T], BF16)
                    Vt = sb(es2, nc, "Vt", [128, 20, 512], BF16)
                    bb = sb(es2, nc, "bb", [128, 8, 640], BF16)
                    lb = Buf()
                    t.dma(t.sp, kT[:], kv_[:, hh * 4:(hh + 1) * 4, :], writes=[lb])
                    t.dma(t.sp, qT[:], qv_[:, hh * 4:(hh + 1) * 4, :], writes=[lb])
                    t.dma(t.sp, Vt[:], vv_[:, :, hh * 512:(hh + 1) * 512], writes=[lb])
                    t.dma(t.pool, bb[:], bv_[:, hh * 8:(hh + 1) * 8, :], writes=[lb])
                    for cc in range(4):
                        ao, aob = aor.next()
                        for qt in range(16):
                            for hi in range(2):
                                hl = 2 * cc + hi
                                pb = 64 * hi
                                S, Sb = cx.ps_pair()

                                def qk():
                                    ins = None
                                    for dd in range(5):
                                        kt = qt + dd
                                        o = S[:, dd * 128:(dd + 1) * 128]
                                        nc.tensor.matmul(o, kT[pb:pb + 64, cc, kt * 128:(kt + 1) * 128],
                                                         qT[pb:pb + 64, cc, qt * 128:(qt + 1) * 128],
                                                         start=True, stop=False)
                                        ins = nc.tensor.matmul(o, ident_bf[:, :], bb[:, hl, dd * 128:(dd + 1) * 128],
                                                               start=False, stop=True)
                                    return ins
                                t.op(t.pe, qk, reads=[lb, cn["b"]], writes=Sb)
                                pt, ptb = ptr.next()
                                t.op(t.act, lambda: nc.scalar.activation(out=pt[:], in_=S[:, 0:640], func=AF.Exp),
                                     reads=Sb, writes=[ptb])
                                N, Nb = cx.ps()
                                num = N[pb:pb + 64, 0:128]
                                den = N[pb:pb + 64, 128:256]

                                def pv():
                                    mm_group(nc, num, [(Vt[:, qt + dd, hl * 64:(hl + 1) * 64], pt[:, dd * 128:(dd + 1) * 128])
                                                       for dd in range(5)])
                                    return mm_group(nc, den, [((ho[:, :] if qt + dd < 4 else ones_bf[:, 0:64]),
                                                               pt[:, dd * 128:(dd + 1) * 128]) for dd in range(5)])
                                t.op(t.pe, pv, reads=[lb, ptb, hob, cn["b"]], writes=[Nb])
                                rc, rcb = rcr.next()
                                t.op(t.dve, lambda: nc.vector.reciprocal(rc[pb:pb + 64, :], den), reads=[Nb], writes=[rcb])
                                t.op(t.dve, lambda: nc.vector.tensor_tensor(out=ao[pb:pb + 64, qt * 128:(qt + 1) * 128],
                                                                            in0=num, in1=rc[pb:pb + 64, :], op=ALU.mult),
                                     reads=[Nb, rcb], writes=[aob])
                        ch = hh * 4 + cc
                        t.dma(t.sp, attn_d[ch * 128:(ch + 1) * 128, :], ao[:], reads=[aob])
                    t.barrier()

        if stop <= 4:
            return nc
        with ExitStack() as es:
            xm = sb(es, nc, "xm", [128, KC, 256], BF16)
            xmb = [Buf()]
            with ExitStack() as es1:
                rmsnorm_phase(cx, es1, memT, [(0, 256)], vecs[:, V_MEMG:V_MEMG + 16], xm, xmb, eps_t, ones_bf)
                t.barrier()
            KmT = sb(es, nc, "KmT", [128, 8, 256], BF16)
            Vm = sb(es, nc, "Vm", [128, 2, 1024], BF16)
            kmb = Buf()
            vmb = Buf()
            wring = Ring(es, nc, "wm", [128, KC, 512], BF16, 2)
            wkv = w_mem_kv.rearrange("(kc p) n -> p kc n", p=128)
            for gi in range(4):
                w, wb = wring.next()
                t.dma(t.pool, w[:], wkv[:, :, gi * 512:(gi + 1) * 512], writes=[wb])
                if gi < 2:
                    for mi in range(4):
                        ps, psb = cx.ps()
                        t.op(t.pe, lambda: mm_group(nc, ps[:, :256], [(w[:, kc, mi * 128:(mi + 1) * 128], xm[:, kc, :])
                                                                       for kc in range(KC)]),
                             reads=[wb, xmb[0]], writes=[psb])
                        t.op(t.dve, lambda: nc.vector.tensor_copy(KmT[:, gi * 4 + mi, :], ps[:, :256]),
                             reads=[psb], writes=[kmb])
                else:
                    for mt in range(2):
                        ps, psb = cx.ps()
                        t.op(t.pe, lambda: mm_group(nc, ps, [(xm[:, kc, mt * 128:(mt + 1) * 128], w[:, kc, :])
                                                              for kc in range(KC)]),
                             reads=[wb, xmb[0]], writes=[psb])
                        t.op(t.dve, lambda: nc.vector.tensor_copy(Vm[:, mt, (gi - 2) * 512:(gi - 1) * 512], ps),
                             reads=[psb], writes=[vmb])
            qm = sb(es, nc, "qm", [128, 8, T], BF16)
            qmb = Buf()
            t.dma(t.sp, qm[:], qm_d.rearrange("(c p) t -> p c t", p=128), writes=[qmb])
            pmr = Ring(es, nc, "pm", [128, 512], BF16, 4)
            rcr = Ring(es, nc, "rcm", [128, 512], F32, 2)
            mor = Ring(es, nc, "mo", [128, T], BF16, 4)
            for h in range(4):
                mo = [mor.next() for _ in range(2)]
                for tq in range(4):
                    pms = []
                    for mt in range(2):
                        S, Sb = cx.ps()
                        t.op(t.pe, lambda: mm_group(nc, S, [(KmT[:, 2 * h + i, mt * 128:(mt + 1) * 128],
                                                             qm[:, 2 * h + i, tq * 512:(tq + 1) * 512]) for i in range(2)]),
                             reads=[kmb, qmb], writes=[Sb])
                        pm, pmb = pmr.next()
                        t.op(t.act, lambda: nc.scalar.activation(out=pm[:], in_=S, func=AF.Exp, scale=1.0 / 16.0),
                             reads=[Sb], writes=[pmb])
                        pms.append((pm, pmb))
                    den, denb = cx.ps()
                    t.op(t.pe, lambda: mm_group(nc, den, [(ones_bf[:, :], pms[mt][0][:]) for mt in range(2)]),
                         reads=[cn["b"], pms[0][1], pms[1][1]], writes=[denb])
                    rc, rcb = rcr.next()
                    t.op(t.dve, lambda: nc.vector.reciprocal(rc[:], den), reads=[denb], writes=[rcb])
                    for i in range(2):
                        num, numb = cx.ps()
                        t.op(t.pe, lambda: mm_group(nc, num, [(Vm[:, mt, (2 * h + i) * 128:(2 * h + i + 1) * 128],
                                                               pms[mt][0][:]) for mt in range(2)]),
                             reads=[vmb, pms[0][1], pms[1][1]], writes=[numb])
                        t.op(t.dve, lambda: nc.vector.tensor_tensor(out=mo[i][0][:, tq * 512:(tq + 1) * 512],
                                                                    in0=num, in1=rc[:], op=ALU.mult),
                             reads=[numb, rcb], writes=[mo[i][1]])
                for i in range(2):
                    ch = 2 * h + i
                    t.dma(t.sp, memo_d[ch * 128:(ch + 1) * 128, :], mo[i][0][:], reads=[mo[i][1]])
            t.barrier()

        if stop <= 5:
            return nc
        with ExitStack() as es:
            br = []
            brb = Buf()
            for nm, dd in (("cact", cact_d), ("attn", attn_d), ("memo", memo_d)):
                x_ = sb(es, nc, nm, [128, 8, T], BF16)
                t.dma(t.sp, x_[:], dd.rearrange("(c p) t -> p c t", p=128), writes=[brb])
                br.append(x_)
            wsrc = [w_conv_out.rearrange("(kc p) n -> p kc n", p=128),
                    w_attn_out.rearrange("(kc p) n -> p kc n", p=128),
                    w_mem_out.rearrange("(kc p) n -> p kc n", p=128)]
            wring = Ring(es, nc, "w5", [128, 3, 8, 512], BF16, 2)
            gring = Ring(es, nc, "g5", [128, 3, T], BF16, 2)
            tr = Ring(es, nc, "t5", [128, 3, 512], F32, 2)
            msr = Ring(es, nc, "ms", [128, T], BF16, 2)
            gv = gates_d.rearrange("(b m p) t -> p b m t", p=128, b=3)
            wl = {}

            def load5(gi):
                w, wb = wring.next()
                for b_ in range(3):
                    t.dma(t.pool, w[:, b_, :, :], wsrc[b_][:, :, gi * 512:(gi + 1) * 512], writes=[wb])
                wl[gi] = (w, wb)
            load5(0)
            for gi in range(4):
                if gi + 1 < 4:
                    load5(gi + 1)
                w, wb = wl.pop(gi)
                for mi in range(4):
                    m = gi * 4 + mi
                    gt, gtb = gring.next()
                    t.dma(t.sp, gt[:], gv[:, :, m, :], writes=[gtb])
                    ms, msb = msr.next()
                    for tt in range(4):
                        tm, tmb = tr.next()
                        for b_ in range(3):
                            ps, psb = cx.ps()
                            t.op(t.pe, lambda: mm_group(nc, ps, [(w[:, b_, kc, mi * 128:(mi + 1) * 128],
                                                                  br[b_][:, kc, tt * 512:(tt + 1) * 512]) for kc in range(8)]),
                                 reads=[wb, brb], writes=[psb])
                            t.op(t.dve, lambda: nc.vector.tensor_tensor(out=tm[:, b_, :], in0=ps,
                                                                        in1=gt[:, b_, tt * 512:(tt + 1) * 512], op=ALU.mult),
                                 reads=[psb, gtb], writes=[tmb])
                        t.op(t.pool, lambda: nc.gpsimd.tensor_tensor(out=tm[:, 0, :], in0=tm[:, 0, :], in1=tm[:, 1, :], op=ALU.add),
                             reads=[tmb], writes=[tmb])
                        t.op(t.pool, lambda: nc.gpsimd.tensor_tensor(out=ms[:, tt * 512:(tt + 1) * 512], in0=tm[:, 0, :],
                                                                     in1=tm[:, 2, :], op=ALU.add),
                             reads=[tmb], writes=[msb])
                    t.dma(t.sp, merged_d[m * 128:(m + 1) * 128, :], ms[:], reads=[msb])
            t.barrier()

        if stop <= 6:
            return nc
        with ExitStack() as es:
            mg = sb(es, nc, "mg", [128, KC, T], BF16)
            mgb = Buf()
            t.dma(t.sp, mg[:], merged_d.rearrange("(c p) t -> p c t", p=128), writes=[mgb])
            wov = w_o.rearrange("(kc p) n -> p kc n", p=128)
            wring = Ring(es, nc, "w6", [128, KC, 512], BF16, 2)
            hr = Ring(es, nc, "hr", [128, T], F32, 2)
            osr = Ring(es, nc, "os", [128, T], F32, 2)
            wl = {}

            def load6(gi):
                w, wb = wring.next()
                t.dma(t.pool, w[:], wov[:, :, gi * 512:(gi + 1) * 512], writes=[wb])
                wl[gi] = (w, wb)
            load6(0)
            for gi in range(4):
                if gi + 1 < 4:
                    load6(gi + 1)
                w, wb = wl.pop(gi)
                for mi in range(4):
                    m = gi * 4 + mi
                    hrt, hrb = hr.next()
                    t.dma(t.sp, hrt[:], hT[m * 128:(m + 1) * 128, HALO:TH], writes=[hrb])
                    os_, osb = osr.next()
                    for tt in range(4):
                        ps, psb = cx.ps()
                        t.op(t.pe, lambda: mm_group(nc, ps, [(w[:, kc, mi * 128:(mi + 1) * 128],
                                                              mg[:, kc, tt * 512:(tt + 1) * 512]) for kc in range(KC)]),
                             reads=[wb, mgb], writes=[psb])
                        t.op(t.dve, lambda: nc.vector.tensor_tensor(out=os_[:, tt * 512:(tt + 1) * 512], in0=ps,
                                                                    in1=hrt[:, tt * 512:(tt + 1) * 512], op=ALU.add),
                             reads=[psb, hrb], writes=[osb])
                    t.dma(t.sp, hmid[m * 128:(m + 1) * 128, :], os_[:], reads=[osb])
            t.barrier()
    return nc


F_G = 0
F_W0 = 16
F_W1 = 16 + 88
F_W2 = 16 + 176
F_B = 16 + 264
F_FING = 16 + 352
NV_FFN = 16 + 352 + 16
FT = 410


def build_ffn(final=False, dbg=False):
    nc = bass.Bass("TRN2", target_bir_lowering=False)
    cx = Ctx(nc)
    t = cx.t

    def din(name, shape):
        return nc.dram_tensor(name, shape, F32, kind="ExternalInput").ap()

    hm = din("hm", [D, T + 2])
    vecs_d = din("vecs", [128, NV_FFN])
    w_up = din("w_up", [D, 2 * D_FF])
    w_down = din("w_down", [D_FF, D])
    out = nc.dram_tensor("out", [D, T], F32, kind="ExternalOutput").ap()
    if dbg:
        act_d = nc.dram_tensor("act_d", [D_FF, T], BF16, kind="ExternalOutput").ap()
    else:
        act_d = nc.dram_tensor("act_d", [D_FF, T], BF16).ap()
    hout_d = nc.dram_tensor("hout_d", [D, T], F32).ap() if final else out

    wup_v = w_up.rearrange("(kc p) n -> p kc n", p=128)
    wdn_v = w_down.rearrange("(kc p) n -> p kc n", p=128)

    with ExitStack() as g:
        vecs = sb(g, nc, "vecs", [128, NV_FFN], F32)
        vb = Buf()
        t.dma(t.sp, vecs[:], vecs_d, writes=[vb])
        cn = load_consts(cx, g, None)
        eps_t, ones_bf = cn["eps"], cn["ones_bf"]
        t.barrier()

        with ExitStack() as es:
            NT = T + 2
            xn = sb(es, nc, "hn", [128, KC, NT], BF16)
            ntile = [(i * FT, FT) for i in range(5)]
            xnb = [Buf() for _ in ntile]
            with ExitStack() as es1:
                rmsnorm_phase(cx, es1, hm, ntile, vecs[:, F_G:F_G + 16], xn, xnb, eps_t, ones_bf)
                t.barrier()
            wring = Ring(es, nc, "wu", [128, 2, KC, 512], BF16, 2)
            tvr = Ring(es, nc, "tv", [128, FT], F32, 3)
            tgr = Ring(es, nc, "tg", [128, FT], F32, 3)
            str_ = Ring(es, nc, "sta", [128, T], BF16, 2)
            wl = {}

            def loadu(gi):
                w, wb = wring.next()
                t.dma(t.pool, w[:, 0, :, :], wup_v[:, :, gi * 512:(gi + 1) * 512], writes=[wb])
                t.dma(t.pool, w[:, 1, :, :], wup_v[:, :, D_FF + gi * 512: D_FF + (gi + 1) * 512], writes=[wb])
                wl[gi] = (w, wb)
            loadu(0)
            for gi in range(11):
                if gi + 1 < 11:
                    loadu(gi + 1)
                w, wb = wl.pop(gi)
                for mi in range(4):
                    c = gi * 4 + mi
                    st, stb = str_.next()
                    for tt in range(5):
                        o0 = tt * FT
                        no = min(FT, T - o0)
                        n = no + 2
                        res = []
                        for half, ring in ((0, tvr), (1, tgr)):
                            ps, psb = cx.ps()
                            t.op(t.pe, lambda: mm_group(nc, ps[:, :n], [(w[:, half, kc, mi * 128:(mi + 1) * 128],
                                                                         xn[:, kc, o0:o0 + n]) for kc in range(KC)]),
                                 reads=[wb, xnb[tt]], writes=[psb])
                            cc = half * NFC + c
                            tv, tvb = ring.next()
                            t.op(t.act, lambda: nc.scalar.activation(out=tv[:, :no], in_=ps[:, 2:2 + no], func=AF.Identity,
                                                                     bias=vecs[:, F_B + cc:F_B + cc + 1],
                                                                     scale=vecs[:, F_W2 + cc:F_W2 + cc + 1]),
                                 reads=[psb, vb], writes=[tvb])
                            t.op(t.dve, lambda: nc.vector.scalar_tensor_tensor(out=tv[:, :no], in0=ps[:, 1:1 + no],
                                                                               scalar=vecs[:, F_W1 + cc:F_W1 + cc + 1],
                                                                               in1=tv[:, :no], op0=ALU.mult, op1=ALU.add),
                                 reads=[psb, tvb, vb], writes=[tvb])
                            t.op(t.dve, lambda: nc.vector.scalar_tensor_tensor(out=tv[:, :no], in0=ps[:, 0:no],
                                                                               scalar=vecs[:, F_W0 + cc:F_W0 + cc + 1],
                                                                               in1=tv[:, :no], op0=ALU.mult, op1=ALU.add),
                                 reads=[psb, tvb, vb], writes=[tvb])
                            res.append((tv, tvb))
                        (tv, tvb), (tg, tgb) = res
                        t.op(t.act, lambda: nc.scalar.activation(out=tg[:, :no], in_=tg[:, :no], func=AF.Silu),
                             reads=[tgb], writes=[tgb])
                        t.op(t.dve, lambda: nc.vector.tensor_tensor(out=st[:, o0:o0 + no], in0=tg[:, :no], in1=tv[:, :no],
                                                                    op=ALU.mult),
                             reads=[tgb, tvb], writes=[stb])
                    t.dma(t.sp, act_d[c * 128:(c + 1) * 128, :], st[:], reads=[stb])
            t.barrier()

        with ExitStack() as es:
            wring = Ring(es, nc, "wd", [128, NFC, 256], BF16, 2)
            hr = Ring(es, nc, "hr", [128, 1024], F32, 2)
            osr = Ring(es, nc, "os", [128, 1024], F32, 2)
            av = act_d.rearrange("(c p) t -> p c t", p=128)
            for hf in range(2):
                with ExitStack() as es2:
                    at = sb(es2, nc, "at", [128, NFC, 1024], BF16)
                    atb = Buf()
                    for q4 in range(4):
                        t.dma(t.sp, at[:, q4 * 11:(q4 + 1) * 11, :], av[:, q4 * 11:(q4 + 1) * 11, hf * 1024:(hf + 1) * 1024],
                              writes=[atb])
                    wl = {}

                    def loadd(gi):
                        w, wb = wring.next()
                        t.dma(t.pool, w[:], wdn_v[:, :, gi * 256:(gi + 1) * 256], writes=[wb])
                        wl[gi] = (w, wb)
                    loadd(0)
                    for gi in range(8):
                        if gi + 1 < 8:
                            loadd(gi + 1)
                        w, wb = wl.pop(gi)
                        for mi in range(2):
                            m = gi * 2 + mi
                            hrt, hrb = hr.next()
                            t.dma(t.sp, hrt[:], hm[m * 128:(m + 1) * 128, 2 + hf * 1024: 2 + (hf + 1) * 1024], writes=[hrb])
                            os_, osb = osr.next()
                            for tt in range(2):
                                ps, psb = cx.ps()
                                t.op(t.pe, lambda: mm_group(nc, ps, [(w[:, kc, mi * 128:(mi + 1) * 128],
                                                                      at[:, kc, tt * 512:(tt + 1) * 512]) for kc in range(NFC)]),
                                     reads=[wb, atb], writes=[psb])
                                t.op(t.dve, lambda: nc.vector.tensor_tensor(out=os_[:, tt * 512:(tt + 1) * 512], in0=ps,
                                                                            in1=hrt[:, tt * 512:(tt + 1) * 512], op=ALU.add),
                                     reads=[psb, hrb], writes=[osb])
                            t.dma(t.sp, hout_d[m * 128:(m + 1) * 128, hf * 1024:(hf + 1) * 1024], os_[:], reads=[osb])
                    t.barrier()

        if final:
            with ExitStack() as es:
                gv = vecs[:, F_FING:F_FING + 16]
                src = hout_d.rearrange("(kc p) t -> p kc t", p=128)
                dst = out.rearrange("(kc p) t -> p kc t", p=128)
                hring = Ring(es, nc, "fh", [128, KC, 512], F32, 2)
                sqring = Ring(es, nc, "fsq", [128, KC, 512], BF16, 2)
                sdring = Ring(es, nc, "fsd", [128, 512], F32, 2)
                rsring = Ring(es, nc, "frs", [128, 512], F32, 2)
                oring = Ring(es, nc, "fo", [128, KC, 512], F32, 2)
                for ti in range(4):
                    j0 = ti * 512
                    hb, hbb = hring.next()
                    t.dma(t.sp, hb[:], src[:, :, j0:j0 + 512], writes=[hbb])
                    sq, sqb = sqring.next()
                    t.op(t.act, lambda: nc.scalar.activation(out=sq[:], in_=hb[:], func=AF.Square), reads=[hbb], writes=[sqb])
                    ps, psb = cx.ps()
                    t.op(t.pe, lambda: mm_group(nc, ps, [(ones_bf[:, :], sq[:, kc, :]) for kc in range(KC)]),
                         reads=[sqb, cn["b"]], writes=[psb])
                    sd, sdb = sdring.next()
                    t.op(t.act, lambda: nc.scalar.activation(out=sd[:], in_=ps, func=AF.Sqrt, bias=eps_t[:, 0:1], scale=1.0 / D),
                         reads=[psb], writes=[sdb])
                    rs, rsb = rsring.next()
                    t.op(t.dve, lambda: nc.vector.reciprocal(rs[:], sd[:]), reads=[sdb], writes=[rsb])
                    o, ob = oring.next()

                    def body():
                        ins = None
                        for kc in range(KC):
                            ins = nc.vector.scalar_tensor_tensor(out=o[:, kc, :], in0=hb[:, kc, :], scalar=gv[:, kc:kc + 1],
                                                                 in1=rs[:], op0=ALU.mult, op1=ALU.mult)
                        return ins
                    t.op(t.dve, body, reads=[hbb, rsb, vb], writes=[ob])
                    t.dma(t.sp, dst[:, :, j0:j0 + 512], o[:], reads=[ob])
                t.barrier()
    return nc


def _cols(v, nch):
    return np.ascontiguousarray(np.asarray(v, np.float32).reshape(nch, 128).T)


def mixer_vecs(inp, l):
    v = np.zeros((128, NV_MIX), np.float32)
    v[:, V_MIXG:V_MIXG + 16] = _cols(inp["mix_norm_g"][l], 16)
    v[:, V_GATEB:V_GATEB + 48] = _cols(inp["gate_b"][l], 48)
    v[:, V_CONVB:V_CONVB + 8] = _cols(inp["conv_b"][l], 8)
    v[:, V_LNG:V_LNG + 8] = _cols(inp["conv_ln_g"][l], 8)
    v[:, V_LNB:V_LNB + 8] = _cols(inp["conv_ln_b"][l], 8)
    cw = np.asarray(inp["conv_w"][l], np.float32)
    cw = cw.reshape(31, 8, 128).transpose(2, 1, 0)
    v[:, V_CONVW:V_CONVW + 248] = cw.reshape(128, 248)
    v[:, V_MEMG:V_MEMG + 16] = _cols(inp["mem_norm_g"][l], 16)
    return v


def ffn_vecs(inp, l):
    v = np.zeros((128, NV_FFN), np.float32)
    v[:, F_G:F_G + 16] = _cols(inp["ffn_norm_g"][l], 16)
    fw = np.asarray(inp["ffn_conv_w"][l], np.float32)
    v[:, F_W0:F_W0 + 88] = _cols(fw[0], 88)
    v[:, F_W1:F_W1 + 88] = _cols(fw[1], 88)
    v[:, F_W2:F_W2 + 88] = _cols(fw[2], 88)
    v[:, F_B:F_B + 88] = _cols(inp["ffn_conv_b"][l], 88)
    v[:, F_FING:F_FING + 16] = _cols(inp["final_norm_g"], 16)
    return v


def bias_blocks(rel_bias_l):
    rb = np.asarray(rel_bias_l, np.float32)
    kl = np.arange(128)[:, None]
    ql = np.arange(128)[None, :]
    out = np.empty((16, 128, 640), np.float32)
    for dd in range(5):
        dist = 128 * (4 - dd) + ql - kl
        idx = np.clip(dist, -256, 256) + 256
        blk = rb[:, idx]
        if dd == 0:
            mask = (kl < 64) & (ql >= 64)
            blk = np.where(mask[None], np.float32(NEG), blk)
        if dd == 4:
            mask = (kl >= 64) & (ql < 64)
            blk = np.where(mask[None], np.float32(NEG), blk)
        out[:, :, dd * 128:(dd + 1) * 128] = blk
    return out


_progs = {}


def _prog(name):
    if name not in _progs:
        if name == "mixer":
            _progs[name] = build_mixer()
        elif name == "ffn":
            _progs[name] = build_ffn(final=False)
        else:
            _progs[name] = build_ffn(final=True)
    return _progs[name]


def _with_halo(hT_full, halo):
    outs = []
    B = hT_full.shape[0]
    for b in range(B):
        for s in range(4):
            t0 = s * T
            a = np.zeros((D, halo + T), np.float32)
            a[:, halo:] = hT_full[b, :, t0:t0 + T]
            if s > 0:
                a[:, :halo] = hT_full[b, :, t0 - halo:t0]
            outs.append(a)
    return outs


def kernel(**inp):
    inp = {k: np.asarray(v) for k, v in inp.items()}
    x = inp["x"].astype(np.float32, copy=False)
    B, S, _ = x.shape
    hT = np.ascontiguousarray(x.transpose(0, 2, 1))
    memT = np.ascontiguousarray(inp["mem"].transpose(0, 2, 1))
    ident = np.eye(128, dtype=np.float32)
    cores = list(range(NCORES))
    for l in range(DEPTH):
        hs = _with_halo(hT, HALO)
        mv = mixer_vecs(inp, l)
        bb = bias_blocks(inp["rel_bias"][l])
        in_maps = []
        for c in cores:
            b, s = divmod(c, 4)
            in_maps.append({
                "hT": hs[c], "memT": memT[b], "vecs": mv,
                "w_in": inp["w_in"][l], "w_conv_out": inp["w_conv_out"][l], "w_attn_out": inp["w_attn_out"][l],
                "w_mem_kv": inp["w_mem_kv"][l], "w_mem_out": inp["w_mem_out"][l], "w_o": inp["w_o"][l],
                "biasblk": bb,
                "halo_ones": np.full((128, 64), 0.0 if s == 0 else 1.0, np.float32),
                "ident": ident,
            })
        res = run_bass_kernel_spmd(_prog("mixer"), in_maps, core_ids=cores)
        hmT = np.stack([np.concatenate([res.results[b * 4 + s]["hmid"] for s in range(4)], axis=1) for b in range(B)])
        hs2 = _with_halo(hmT, 2)
        fv = ffn_vecs(inp, l)
        in_maps = [{"hm": hs2[c], "vecs": fv, "w_up": inp["w_up"][l], "w_down": inp["w_down"][l]} for c in cores]
        res = run_bass_kernel_spmd(_prog("ffn_final" if l == DEPTH - 1 else "ffn"), in_maps, core_ids=cores)
        hT = np.stack([np.concatenate([res.results[b * 4 + s]["out"] for s in range(4)], axis=1) for b in range(B)])
    return np.ascontiguousarray(hT.transpose(0, 2, 1)).astype(np.float32)
```

```python
from contextlib import ExitStack
import numpy as np
import concourse.bass as bass
import concourse.mybir as mybir
from concourse.bass_utils import run_bass_kernel_spmd

F32 = mybir.dt.float32
BF16 = mybir.dt.bfloat16
AF = mybir.ActivationFunctionType
ALU = mybir.AluOpType

D = 2048
KC = 16
T = 2048
HALO = 512
TH = T + HALO
DEPTH = 4
NCORES = 8
D_FF = 5632
NFC = 44
EPS = 1e-6
NEG = -30000.0


class Buf:
    __slots__ = ("last_w", "readers")

    def __init__(self):
        self.last_w = None
        self.readers = {}


class Eng:
    def __init__(self, nc, name, handle):
        self.name = name
        self.h = handle
        self.sem = nc.alloc_semaphore("s_" + name)
        self.count = 0
        self.seen = {}


class Trk:
    def __init__(self, nc, n_dma_sems=32):
        self.nc = nc
        self.pe = Eng(nc, "pe", nc.tensor)
        self.act = Eng(nc, "act", nc.scalar)
        self.dve = Eng(nc, "dve", nc.vector)
        self.pool = Eng(nc, "pool", nc.gpsimd)
        self.sp = Eng(nc, "sp", nc.sync)
        self.engs = [self.pe, self.act, self.dve, self.pool, self.sp]
        self.dsems = [[nc.alloc_semaphore(f"d{i}"), 0] for i in range(n_dma_sems)]
        self.dnext = 0

    def _wait(self, eng, toks):
        best = {}
        for (sem, val, key) in toks:
            if key not in best or best[key][1] < val:
                best[key] = (sem, val)
        for key, (sem, val) in best.items():
            if eng.seen.get(key, 0) >= val:
                continue
            eng.h.wait_ge(sem, val)
            eng.seen[key] = val

    def _deps(self, eng, reads, writes):
        toks = []
        for b in reads:
            if b.last_w is not None:
                if not (b.last_w[2] == eng.name and eng.name == "pe"):
                    toks.append(b.last_w)
        for b in writes:
            if b.last_w is not None and b.last_w[2] != eng.name:
                toks.append(b.last_w)
            for k, tk in b.readers.items():
                if k != eng.name:
                    toks.append(tk)
        return toks

    def op(self, eng, fn, reads=(), writes=()):
        self._wait(eng, self._deps(eng, reads, writes))
        ins = fn()
        eng.count += 1
        ins.then_inc(eng.sem, 1)
        tok = (eng.sem, eng.count, eng.name)
        for b in reads:
            b.readers[eng.name] = tok
        for b in writes:
            b.last_w = tok
            b.readers = {}
        return ins

    def dma(self, eng, out, in_, reads=(), writes=(), **kw):
        toks = self._deps(eng, reads, writes)
        slot = self.dsems[self.dnext]
        key = f"dma{self.dnext}"
        self.dnext = (self.dnext + 1) % len(self.dsems)
        if slot[1] > 0:
            toks.append((slot[0], slot[1], key))
        self._wait(eng, toks)
        ins = eng.h.dma_start(out=out, in_=in_, **kw)
        slot[1] += 16
        ins.then_inc(slot[0], 16)
        tok = (slot[0], slot[1], key)
        for b in reads:
            b.readers[key] = tok
        for b in writes:
            b.last_w = tok
            b.readers = {}
        return ins

    def barrier(self):
        toks = [(e.sem, e.count, e.name) for e in self.engs if e.count > 0]
        toks += [(s[0], s[1], f"dma{i}") for i, s in enumerate(self.dsems) if s[1] > 0]
        for e in self.engs:
            self._wait(e, [tk for tk in toks if tk[2] != e.name])


_uid = [0]


def uid():
    _uid[0] += 1
    return _uid[0]


class Ring:
    def __init__(self, es, nc, name, shape, dtype, n):
        self.t = [es.enter_context(nc.sbuf_tensor(f"{name}_{i}_{uid()}", shape, dtype)) for i in range(n)]
        self.b = [Buf() for _ in range(n)]
        self.i = 0

    def next(self):
        k = self.i % len(self.t)
        self.i += 1
        return self.t[k], self.b[k]


class Ctx:
    def __init__(self, nc):
        self.nc = nc
        self.t = Trk(nc)
        self.psall = nc.alloc_psum_tensor("psall", [128, 8, 512], F32)
        self.psb = [Buf() for _ in range(8)]
        self.psi = 0
        self.ev = 0

    def ps(self):
        k = self.psi % 8
        self.psi += 1
        return self.psall[:, k, :], self.psb[k]

    def ps_pair(self):
        if self.psi % 2:
            self.psi += 1
        k = self.psi % 8
        self.psi += 2
        flat = self.psall[:, k:k + 2, :].rearrange("p a b -> p (a b)")
        return flat, [self.psb[k], self.psb[k + 1]]


def mm_group(nc, out, pairs):
    n = len(pairs)
    ins = None
    for i, (l, r) in enumerate(pairs):
        ins = nc.tensor.matmul(out, l, r, start=(i == 0), stop=(i == n - 1))
    return ins


def sb(es, nc, name, shape, dtype):
    return es.enter_context(nc.sbuf_tensor(f"{name}_{uid()}", shape, dtype))


def rmsnorm_phase(cx, es, srcT, tiles, gvec, xn, xnb, eps_t, ones_bf):
    nc, t = cx.nc, cx.t
    src = srcT.rearrange("(kc p) t -> p kc t", p=128)
    hring = Ring(es, nc, "nh", [128, KC, 512], F32, 2)
    sqring = Ring(es, nc, "nsq", [128, KC, 512], BF16, 2)
    sdring = Ring(es, nc, "nsd", [128, 512], F32, 2)
    rsring = Ring(es, nc, "nrs", [128, 512], F32, 2)
    for ti, (j0, n) in enumerate(tiles):
        hb, hbb = hring.next()
        t.dma(t.sp, hb[:, :, :n], src[:, :, j0:j0 + n], writes=[hbb])
        sq, sqb = sqring.next()
        t.op(t.act, lambda: nc.scalar.activation(out=sq[:, :, :n], in_=hb[:, :, :n], func=AF.Square),
             reads=[hbb], writes=[sqb])
        ps, psb = cx.ps()
        t.op(t.pe, lambda: mm_group(nc, ps[:, :n], [(ones_bf[:, :], sq[:, kc, :n]) for kc in range(KC)]),
             reads=[sqb], writes=[psb])
        sd, sdb = sdring.next()
        t.op(t.act, lambda: nc.scalar.activation(out=sd[:, :n], in_=ps[:, :n], func=AF.Sqrt,
                                                 bias=eps_t[:, 0:1], scale=1.0 / D),
             reads=[psb], writes=[sdb])
        rs, rsb = rsring.next()
        t.op(t.dve, lambda: nc.vector.reciprocal(rs[:, :n], sd[:, :n]), reads=[sdb], writes=[rsb])

        def body():
            ins = None
            for kc in range(KC):
                ins = nc.vector.scalar_tensor_tensor(out=xn[:, kc, j0:j0 + n], in0=hb[:, kc, :n],
                                                     scalar=gvec[:, kc:kc + 1], in1=rs[:, :n],
                                                     op0=ALU.mult, op1=ALU.mult)
            return ins
        t.op(t.dve, body, reads=[hbb, rsb], writes=[xnb[ti]])


def load_consts(cx, es, ident_d):
    nc, t = cx.nc, cx.t
    c = {}
    c["eps"] = sb(es, nc, "eps", [128, 1], F32)
    c["ones_bf"] = sb(es, nc, "ones_bf", [128, 128], BF16)
    c["b"] = Buf()
    t.op(t.dve, lambda: nc.vector.memset(c["eps"][:], EPS), writes=[c["b"]])
    t.op(t.dve, lambda: nc.vector.memset(c["ones_bf"][:], 1.0), writes=[c["b"]])
    return c


V_MIXG = 0
V_GATEB = 16
V_CONVB = 64
V_LNG = 72
V_LNB = 80
V_CONVW = 88
V_MEMG = 88 + 248
NV_MIX = 352


def build_mixer(dbg=False, stop=99):
    nc = bass.Bass("TRN2", target_bir_lowering=False)
    cx = Ctx(nc)
    t = cx.t

    def din(name, shape):
        return nc.dram_tensor(name, shape, F32, kind="ExternalInput").ap()

    def dscr(name, shape, dt):
        if dbg:
            return nc.dram_tensor(name, shape, dt, kind="ExternalOutput").ap()
        return nc.dram_tensor(name, shape, dt).ap()

    hT = din("hT", [D, TH])
    memT = din("memT", [D, 256])
    vecs_d = din("vecs", [128, NV_MIX])
    w_in = din("w_in", [D, 12288])
    w_conv_out = din("w_conv_out", [1024, D])
    w_attn_out = din("w_attn_out", [1024, D])
    w_mem_kv = din("w_mem_kv", [D, 2048])
    w_mem_out = din("w_mem_out", [1024, D])
    w_o = din("w_o", [D, D])
    biasblk = din("biasblk", [16, 128, 640])
    halo_ones_d = din("halo_ones", [128, 64])
    ident_d = din("ident", [128, 128])
    hmid = nc.dram_tensor("hmid", [D, T], F32, kind="ExternalOutput").ap()

    qT_d = dscr("qT_d", [1024, T], BF16)
    kT_d = dscr("kT_d", [1024, TH], BF16)
    V_d = dscr("V_d", [TH, 1024], BF16)
    qm_d = dscr("qm_d", [1024, T], BF16)
    gates_d = dscr("gates_d", [6144, T], BF16)
    hglu_d = dscr("hglu_d", [1024, 2080], BF16)
    cact_d = dscr("cact_d", [1024, T], BF16)
    attn_d = dscr("attn_d", [1024, T], BF16)
    memo_d = dscr("memo_d", [1024, T], BF16)
    merged_d = dscr("merged_d", [D, T], BF16)

    win_v = w_in.rearrange("(kc p) n -> p kc n", p=128)

    with ExitStack() as g:
        vecs = sb(g, nc, "vecs", [128, NV_MIX], F32)
        vb = Buf()
        t.dma(t.sp, vecs[:], vecs_d, writes=[vb])
        cn = load_consts(cx, g, ident_d)
        eps_t, ones_bf = cn["eps"], cn["ones_bf"]
        ident_bf = sb(g, nc, "ident_bf", [128, 128], BF16)
        ident_f = sb(g, nc, "ident_f", [128, 128], F32)
        t.dma(t.pool, ident_bf[:], ident_d, writes=[cn["b"]])
        t.dma(t.sp, ident_f[:], ident_d, writes=[cn["b"]])
        t.barrier()
        if stop <= 0:
            return nc

        with ExitStack() as es:
            xn = sb(es, nc, "xn", [128, KC, TH], BF16)
            ntile = [(i * 512, 512) for i in range(5)]
            xnb = [Buf() for _ in ntile]
            with ExitStack() as es1:
                rmsnorm_phase(cx, es1, hT, ntile, vecs[:, V_MIXG:V_MIXG + 16], xn, xnb, eps_t, ones_bf)
                t.barrier()
                if stop <= 1:
                    return nc

            def xbufs(j0, n):
                return [xnb[i] for i in range(5) if i * 512 < j0 + n and (i + 1) * 512 > j0]

            wring = Ring(es, nc, "w", [128, KC, 512], BF16, 3)
            stq = Ring(es, nc, "stq", [128, TH], BF16, 2)
            stv = Ring(es, nc, "stv", [128, 512], BF16, 3)
            sgr = Ring(es, nc, "sg", [128, 416], F32, 2)

            groups = [("glu", i) for i in range(4)] + [("q", i) for i in range(2)] + \
                     [("k", i) for i in range(2)] + [("v", i) for i in range(2)] + \
                     [("qm", i) for i in range(2)] + [("gate", i) for i in range(12)]
            loaded = {}

            def load(gi):
                kind, i = groups[gi]
                w, wb = wring.next()
                if kind == "glu":
                    t.dma(t.pool, w[:, :, 0:256], win_v[:, :, i * 256:(i + 1) * 256], writes=[wb])
                    t.dma(t.pool, w[:, :, 256:512], win_v[:, :, 1024 + i * 256:1024 + (i + 1) * 256], writes=[wb])
                else:
                    base = {"q": 2048, "k": 3072, "v": 4096, "qm": 5120, "gate": 6144}[kind]
                    t.dma(t.pool, w[:], win_v[:, :, base + i * 512: base + (i + 1) * 512], writes=[wb])
                loaded[gi] = (w, wb)

            def evac_copy(dst, ps, psb, dstb, scale=None):
                cx.ev += 1
                if cx.ev % 2 == 0:
                    if scale is None:
                        t.op(t.act, lambda: nc.scalar.activation(out=dst, in_=ps, func=AF.Copy),
                             reads=[psb], writes=[dstb])
                    else:
                        t.op(t.act, lambda: nc.scalar.activation(out=dst, in_=ps, func=AF.Copy, scale=scale),
                             reads=[psb], writes=[dstb])
                else:
                    if scale is None:
                        t.op(t.dve, lambda: nc.vector.tensor_copy(dst, ps), reads=[psb], writes=[dstb])
                    else:
                        t.op(t.dve, lambda: nc.vector.tensor_scalar(out=dst, in0=ps, scalar1=scale, scalar2=None,
                                                                    op0=ALU.mult),
                             reads=[psb], writes=[dstb])

            load(0)
            load(1)
            for gi, (kind, i) in enumerate(groups):
                if gi + 2 < len(groups):
                    load(gi + 2)
                w, wb = loaded.pop(gi)
                if kind in ("q", "k", "qm", "gate"):
                    j_lo = 0 if kind == "k" else HALO
                    ntt = 5 if kind == "k" else 4
                    for mi in range(4):
                        m = i * 4 + mi
                        st, stb = stq.next()
                        for tt in range(ntt):
                            j0 = j_lo + tt * 512
                            ps, psb = cx.ps()
                            t.op(t.pe, lambda: mm_group(nc, ps, [(w[:, kc, mi * 128:(mi + 1) * 128],
                                                                  xn[:, kc, j0:j0 + 512]) for kc in range(KC)]),
                                 reads=[wb] + xbufs(j0, 512), writes=[psb])
                            dst = st[:, tt * 512:(tt + 1) * 512]
                            if kind == "gate":
                                t.op(t.act, lambda: nc.scalar.activation(
                                    out=dst, in_=ps, func=AF.Sigmoid,
                                    bias=vecs[:, V_GATEB + m:V_GATEB + m + 1], scale=1.0),
                                    reads=[psb, vb], writes=[stb])
                            elif kind == "q":
                                evac_copy(dst, ps, psb, stb, scale=0.125)
                            else:
                                evac_copy(dst, ps, psb, stb)
                        dd = {"q": qT_d, "k": kT_d, "qm": qm_d, "gate": gates_d}[kind]
                        t.dma(t.sp, dd[m * 128:(m + 1) * 128, :], st[:, :ntt * 512], reads=[stb])
                elif kind == "v":
                    for tk in range(20):
                        ps, psb = cx.ps()
                        t.op(t.pe, lambda: mm_group(nc, ps, [(xn[:, kc, tk * 128:(tk + 1) * 128], w[:, kc, :])
                                                              for kc in range(KC)]),
                             reads=[wb] + xbufs(tk * 128, 128), writes=[psb])
                        st, stb = stv.next()
                        evac_copy(st[:], ps, psb, stb)
                        t.dma(t.sp, V_d[tk * 128:(tk + 1) * 128, i * 512:(i + 1) * 512], st[:], reads=[stb])
                else:
                    for ci in range(2):
                        c = 2 * i + ci
                        st, stb = stq.next()
                        for tt in range(5):
                            j0 = 480 + tt * 416
                            psa, psab = cx.ps()
                            psg, psgb = cx.ps()
                            t.op(t.pe, lambda: mm_group(nc, psa[:, :416], [(w[:, kc, ci * 128:(ci + 1) * 128],
                                                                            xn[:, kc, j0:j0 + 416]) for kc in range(KC)]),
                                 reads=[wb] + xbufs(j0, 416), writes=[psab])
                            t.op(t.pe, lambda: mm_group(nc, psg[:, :416], [(w[:, kc, 256 + ci * 128:256 + (ci + 1) * 128],
                                                                            xn[:, kc, j0:j0 + 416]) for kc in range(KC)]),
                                 reads=[wb] + xbufs(j0, 416), writes=[psgb])
                            sg, sgb = sgr.next()
                            t.op(t.act, lambda: nc.scalar.activation(out=sg[:], in_=psg[:, :416], func=AF.Sigmoid),
                                 reads=[psgb], writes=[sgb])
                            t.op(t.dve, lambda: nc.vector.tensor_tensor(out=st[:, tt * 416:(tt + 1) * 416],
                                                                        in0=psa[:, :416], in1=sg[:], op=ALU.mult),
                                 reads=[psab, sgb], writes=[stb])
                        t.dma(t.sp, hglu_d[c * 128:(c + 1) * 128, :], st[:, :2080], reads=[stb])
            t.barrier()

        if stop <= 2:
            return nc
        with ExitStack() as es:
            hg = sb(es, nc, "hg", [128, 8, 2080], BF16)
            hgb = [Buf() for _ in range(8)]
            hv = hglu_d.rearrange("(c p) t -> p c t", p=128)
            for c in range(8):
                t.dma(t.sp, hg[:, c, :], hv[:, c, :], writes=[hgb[c]])
            dg = sb(es, nc, "dg", [128, 8, 31, 128], BF16)
            dgb = [Buf() for _ in range(8)]
            for c in range(8):
                def mk():
                    ins = None
                    for k in range(31):
                        ins = nc.vector.tensor_scalar(out=dg[:, c, k, :], in0=ident_f[:, :],
                                                      scalar1=vecs[:, V_CONVW + c * 31 + k:V_CONVW + c * 31 + k + 1],
                                                      scalar2=None, op0=ALU.mult)
                    return ins
                t.op(t.dve, mk, reads=[vb], writes=[dgb[c]])
            onesS = sb(es, nc, "onesS", [128, 128], BF16)
            osb = Buf()
            t.op(t.dve, lambda: nc.vector.memset(onesS[:], 1.0 / 1024.0), writes=[osb])
            yr = Ring(es, nc, "y", [128, 8, 512], F32, 2)
            ybr = Ring(es, nc, "yb", [128, 8, 512], BF16, 2)
            ysr = Ring(es, nc, "ys", [128, 8, 512], BF16, 2)
            smr = Ring(es, nc, "sm", [128, 512], F32, 2)
            sqr_ = Ring(es, nc, "sq", [128, 512], F32, 2)
            srr = Ring(es, nc, "sr", [128, 512], F32, 2)
            tmr = Ring(es, nc, "tm", [128, 512], F32, 3)
            cst = Ring(es, nc, "cst", [128, 512], BF16, 3)
            for tt in range(4):
                y, yb = yr.next()
                ybf, ybfb = ybr.next()
                ysq, ysqb = ysr.next()
                for c in range(8):
                    ps, psb = cx.ps()
                    t.op(t.pe, lambda: mm_group(nc, ps, [(dg[:, c, k, :], hg[:, c, tt * 512 + 2 + k: tt * 512 + 2 + k + 512])
                                                          for k in range(31)]),
                         reads=[dgb[c], hgb[c]], writes=[psb])
                    cb = vecs[:, V_CONVB + c:V_CONVB + c + 1]
                    t.op(t.act, lambda: nc.scalar.activation(out=y[:, c, :], in_=ps, func=AF.Identity, bias=cb, scale=1.0),
                         reads=[psb, vb], writes=[yb])
                    t.op(t.dve, lambda: nc.vector.tensor_copy(ybf[:, c, :], y[:, c, :]),
                         reads=[yb], writes=[ybfb])
                    t.op(t.act, lambda: nc.scalar.activation(out=ysq[:, c, :], in_=ps, func=AF.Square, bias=cb, scale=1.0),
                         reads=[psb, vb], writes=[ysqb])
                pm, pmb = cx.ps()
                t.op(t.pe, lambda: mm_group(nc, pm, [(onesS[:, :], ybf[:, c, :]) for c in range(8)]),
                     reads=[osb, ybfb], writes=[pmb])
                pe2, pe2b = cx.ps()
                t.op(t.pe, lambda: mm_group(nc, pe2, [(onesS[:, :], ysq[:, c, :]) for c in range(8)]),
                     reads=[osb, ysqb], writes=[pe2b])
                sm, smb = smr.next()
                sq, sqb = sqr_.next()
                sr, srb = srr.next()
                t.op(t.act, lambda: nc.scalar.activation(out=sm[:], in_=pm, func=AF.Copy), reads=[pmb], writes=[smb])
                t.op(t.act, lambda: nc.scalar.activation(out=sq[:], in_=pm, func=AF.Square), reads=[pmb], writes=[sqb])
                t.op(t.dve, lambda: nc.vector.tensor_tensor(out=sq[:], in0=pe2, in1=sq[:], op=ALU.subtract),
                     reads=[pe2b, sqb], writes=[sqb])
                t.op(t.act, lambda: nc.scalar.activation(out=sq[:], in_=sq[:], func=AF.Sqrt, bias=eps_t[:, 0:1], scale=1.0),
                     reads=[sqb], writes=[sqb])
                t.op(t.dve, lambda: nc.vector.reciprocal(sr[:], sq[:]), reads=[sqb], writes=[srb])
                for c in range(8):
                    tm, tmb = tmr.next()
                    t.op(t.dve, lambda: nc.vector.tensor_tensor(out=tm[:], in0=y[:, c, :], in1=sm[:], op=ALU.subtract),
                         reads=[yb, smb], writes=[tmb])
                    t.op(t.dve, lambda: nc.vector.tensor_tensor(out=tm[:], in0=tm[:], in1=sr[:], op=ALU.mult),
                         reads=[tmb, srb], writes=[tmb])
                    cs, csb = cst.next()
                    t.op(t.act, lambda: nc.scalar.activation(out=cs[:], in_=tm[:], func=AF.Silu,
                                                             bias=vecs[:, V_LNB + c:V_LNB + c + 1],
                                                             scale=vecs[:, V_LNG + c:V_LNG + c + 1]),
                         reads=[tmb, vb], writes=[csb])
                    t.dma(t.sp, cact_d[c * 128:(c + 1) * 128, tt * 512:(tt + 1) * 512], cs[:], reads=[csb])
            t.barrier()

        if stop <= 3:
            return nc
        with ExitStack() as es:
            ho = sb(es, nc, "halo_ones", [128, 64], BF16)
            hob = Buf()
            t.dma(t.pool, ho[:], halo_ones_d, writes=[hob])
            kv_ = kT_d.rearrange("(c p) t -> p c t", p=128)
            qv_ = qT_d.rearrange("(c p) t -> p c t", p=128)
            vv_ = V_d.rearrange("(k p) c -> p k c", p=128)
            bv_ = biasblk.rearrange("h p c -> p h c")
            ptr = Ring(es, nc, "pt", [128, 640], BF16, 3)
            rcr = Ring(es, nc, "rc", [128, 128], F32, 3)
            aor = Ring(es, nc, "ao", [128, T], BF16, 2)
            for hh in range(2):
                with ExitStack() as es2:
                    kT = sb(es2, nc, "kT", [128, 4, TH], BF16)
                    qT = sb(es2, nc, "qT", [128, 4, T], BF16)
                    Vt = sb(es2, nc, "Vt", [128, 20, 512], BF16)
                    bb = sb(es2, nc, "bb", [128, 8, 640], BF16)
                    lb = Buf()
                    t.dma(t.sp, kT[:], kv_[:, hh * 4:(hh + 1) * 4, :], writes=[lb])
                    t.dma(t.sp, qT[:], qv_[:, hh * 4:(hh + 1) * 4, :], writes=[lb])
                    t.dma(t.sp, Vt[:], vv_[:, :, hh * 512:(hh + 1) * 512], writes=[lb])
                    t.dma(t.pool, bb[:], bv_[:, hh * 8:(hh + 1) * 8, :], writes=[lb])
                    for cc in range(4):
                        ao, aob = aor.next()
                        items = [(qt, hi) for qt in range(16) for hi in range(2)]

                        def emit_qk(qt, hi):
                            hl = 2 * cc + hi
                            pb = 64 * hi
                            S, Sb = cx.ps_pair()

                            def qk():
                                ins = None
                                for dd in range(5):
                                    kt = qt + dd
                                    o = S[:, dd * 128:(dd + 1) * 128]
                                    nc.tensor.matmul(o, kT[pb:pb + 64, cc, kt * 128:(kt + 1) * 128],
                                                     qT[pb:pb + 64, cc, qt * 128:(qt + 1) * 128],
                                                     start=True, stop=False)
                                    ins = nc.tensor.matmul(o, ident_bf[:, :], bb[:, hl, dd * 128:(dd + 1) * 128],
                                                           start=False, stop=True)
                                return ins
                            t.op(t.pe, qk, reads=[lb, cn["b"]], writes=Sb)
                            return S, Sb

                        nxt = emit_qk(*items[0])
                        for ii, (qt, hi) in enumerate(items):
                            hl = 2 * cc + hi
                            pb = 64 * hi
                            S, Sb = nxt
                            if ii + 1 < len(items):
                                nxt = emit_qk(*items[ii + 1])
                            pt, ptb = ptr.next()
                            t.op(t.act, lambda: nc.scalar.activation(out=pt[:], in_=S[:, 0:640], func=AF.Exp),
                                 reads=Sb, writes=[ptb])
                            N, Nb = cx.ps()
                            num = N[pb:pb + 64, 0:128]
                            den = N[pb:pb + 64, 128:256]

                            def pv():
                                mm_group(nc, num, [(Vt[:, qt + dd, hl * 64:(hl + 1) * 64], pt[:, dd * 128:(dd + 1) * 128])
                                                   for dd in range(5)])
                                return mm_group(nc, den, [((ho[:, :] if qt + dd < 4 else ones_bf[:, 0:64]),
                                                           pt[:, dd * 128:(dd + 1) * 128]) for dd in range(5)])
                            t.op(t.pe, pv, reads=[lb, ptb, hob, cn["b"]], writes=[Nb])
                            rc, rcb = rcr.next()
                            t.op(t.dve, lambda: nc.vector.reciprocal(rc[pb:pb + 64, :], den), reads=[Nb], writes=[rcb])
                            t.op(t.dve, lambda: nc.vector.tensor_tensor(out=ao[pb:pb + 64, qt * 128:(qt + 1) * 128],
                                                                        in0=num, in1=rc[pb:pb + 64, :], op=ALU.mult),
                                 reads=[Nb, rcb], writes=[aob])
                        ch = hh * 4 + cc
                        t.dma(t.sp, attn_d[ch * 128:(ch + 1) * 128, :], ao[:], reads=[aob])
                    t.barrier()

        if stop <= 4:
            return nc
        with ExitStack() as es:
            xm = sb(es, nc, "xm", [128, KC, 256], BF16)
            xmb = [Buf()]
            with ExitStack() as es1:
                rmsnorm_phase(cx, es1, memT, [(0, 256)], vecs[:, V_MEMG:V_MEMG + 16], xm, xmb, eps_t, ones_bf)
                t.barrier()
            KmT = sb(es, nc, "KmT", [128, 8, 256], BF16)
            Vm = sb(es, nc, "Vm", [128, 2, 1024], BF16)
            kmb = Buf()
            vmb = Buf()
            wring = Ring(es, nc, "wm", [128, KC, 512], BF16, 2)
            wkv = w_mem_kv.rearrange("(kc p) n -> p kc n", p=128)
            for gi in range(4):
                w, wb = wring.next()
                t.dma(t.pool, w[:], wkv[:, :, gi * 512:(gi + 1) * 512], writes=[wb])
                if gi < 2:
                    for mi in range(4):
                        ps, psb = cx.ps()
                        t.op(t.pe, lambda: mm_group(nc, ps[:, :256], [(w[:, kc, mi * 128:(mi + 1) * 128], xm[:, kc, :])
                                                                       for kc in range(KC)]),
                             reads=[wb, xmb[0]], writes=[psb])
                        t.op(t.dve, lambda: nc.vector.tensor_copy(KmT[:, gi * 4 + mi, :], ps[:, :256]),
                             reads=[psb], writes=[kmb])
                else:
                    for mt in range(2):
                        ps, psb = cx.ps()
                        t.op(t.pe, lambda: mm_group(nc, ps, [(xm[:, kc, mt * 128:(mt + 1) * 128], w[:, kc, :])
                                                              for kc in range(KC)]),
                             reads=[wb, xmb[0]], writes=[psb])
                        t.op(t.dve, lambda: nc.vector.tensor_copy(Vm[:, mt, (gi - 2) * 512:(gi - 1) * 512], ps),
                             reads=[psb], writes=[vmb])
            qm = sb(es, nc, "qm", [128, 8, T], BF16)
            qmb = Buf()
            t.dma(t.sp, qm[:], qm_d.rearrange("(c p) t -> p c t", p=128), writes=[qmb])
            pmr = Ring(es, nc, "pm", [128, 512], BF16, 4)
            rcr = Ring(es, nc, "rcm", [128, 512], F32, 2)
            mor = Ring(es, nc, "mo", [128, T], BF16, 4)
            for h in range(4):
                mo = [mor.next() for _ in range(2)]
                for tq in range(4):
                    pms = []
                    for mt in range(2):
                        S, Sb = cx.ps()
                        t.op(t.pe, lambda: mm_group(nc, S, [(KmT[:, 2 * h + i, mt * 128:(mt + 1) * 128],
                                                             qm[:, 2 * h + i, tq * 512:(tq + 1) * 512]) for i in range(2)]),
                             reads=[kmb, qmb], writes=[Sb])
                        pm, pmb = pmr.next()
                        t.op(t.act, lambda: nc.scalar.activation(out=pm[:], in_=S, func=AF.Exp, scale=1.0 / 16.0),
                             reads=[Sb], writes=[pmb])
                        pms.append((pm, pmb))
                    den, denb = cx.ps()
                    t.op(t.pe, lambda: mm_group(nc, den, [(ones_bf[:, :], pms[mt][0][:]) for mt in range(2)]),
                         reads=[cn["b"], pms[0][1], pms[1][1]], writes=[denb])
                    rc, rcb = rcr.next()
                    t.op(t.dve, lambda: nc.vector.reciprocal(rc[:], den), reads=[denb], writes=[rcb])
                    for i in range(2):
                        num, numb = cx.ps()
                        t.op(t.pe, lambda: mm_group(nc, num, [(Vm[:, mt, (2 * h + i) * 128:(2 * h + i + 1) * 128],
                                                               pms[mt][0][:]) for mt in range(2)]),
                             reads=[vmb, pms[0][1], pms[1][1]], writes=[numb])
                        t.op(t.dve, lambda: nc.vector.tensor_tensor(out=mo[i][0][:, tq * 512:(tq + 1) * 512],
                                                                    in0=num, in1=rc[:], op=ALU.mult),
                             reads=[numb, rcb], writes=[mo[i][1]])
                for i in range(2):
                    ch = 2 * h + i
                    t.dma(t.sp, memo_d[ch * 128:(ch + 1) * 128, :], mo[i][0][:], reads=[mo[i][1]])
            t.barrier()

        if stop <= 5:
            return nc
        with ExitStack() as es:
            br = []
            brb = Buf()
            for nm, dd in (("cact", cact_d), ("attn", attn_d), ("memo", memo_d)):
                x_ = sb(es, nc, nm, [128, 8, T], BF16)
                t.dma(t.sp, x_[:], dd.rearrange("(c p) t -> p c t", p=128), writes=[brb])
                br.append(x_)
            wsrc = [w_conv_out.rearrange("(kc p) n -> p kc n", p=128),
                    w_attn_out.rearrange("(kc p) n -> p kc n", p=128),
                    w_mem_out.rearrange("(kc p) n -> p kc n", p=128)]
            wring = Ring(es, nc, "w5", [128, 3, 8, 512], BF16, 2)
            gring = Ring(es, nc, "g5", [128, 3, T], BF16, 2)
            tr = Ring(es, nc, "t5", [128, 3, 512], F32, 2)
            msr = Ring(es, nc, "ms", [128, T], BF16, 2)
            gv = gates_d.rearrange("(b m p) t -> p b m t", p=128, b=3)
            wl = {}

            def load5(gi):
                w, wb = wring.next()
                for b_ in range(3):
                    t.dma(t.pool, w[:, b_, :, :], wsrc[b_][:, :, gi * 512:(gi + 1) * 512], writes=[wb])
                wl[gi] = (w, wb)
            load5(0)
            for gi in range(4):
                if gi + 1 < 4:
                    load5(gi + 1)
                w, wb = wl.pop(gi)
                for mi in range(4):
                    m = gi * 4 + mi
                    gt, gtb = gring.next()
                    t.dma(t.sp, gt[:], gv[:, :, m, :], writes=[gtb])
                    ms, msb = msr.next()
                    for tt in range(4):
                        tm, tmb = tr.next()
                        for b_ in range(3):
                            ps, psb = cx.ps()
                            t.op(t.pe, lambda: mm_group(nc, ps, [(w[:, b_, kc, mi * 128:(mi + 1) * 128],
                                                                  br[b_][:, kc, tt * 512:(tt + 1) * 512]) for kc in range(8)]),
                                 reads=[wb, brb], writes=[psb])
                            t.op(t.dve, lambda: nc.vector.tensor_tensor(out=tm[:, b_, :], in0=ps,
                                                                        in1=gt[:, b_, tt * 512:(tt + 1) * 512], op=ALU.mult),
                                 reads=[psb, gtb], writes=[tmb])
                        t.op(t.pool, lambda: nc.gpsimd.tensor_tensor(out=tm[:, 0, :], in0=tm[:, 0, :], in1=tm[:, 1, :], op=ALU.add),
                             reads=[tmb], writes=[tmb])
                        t.op(t.pool, lambda: nc.gpsimd.tensor_tensor(out=ms[:, tt * 512:(tt + 1) * 512], in0=tm[:, 0, :],
                                                                     in1=tm[:, 2, :], op=ALU.add),
                             reads=[tmb], writes=[msb])
                    t.dma(t.sp, merged_d[m * 128:(m + 1) * 128, :], ms[:], reads=[msb])
            t.barrier()

        if stop <= 6:
            return nc
        with ExitStack() as es:
            mg = sb(es, nc, "mg", [128, KC, T], BF16)
            mgb = Buf()
            t.dma(t.sp, mg[:], merged_d.rearrange("(c p) t -> p c t", p=128), writes=[mgb])
            wov = w_o.rearrange("(kc p) n -> p kc n", p=128)
            wring = Ring(es, nc, "w6", [128, KC, 512], BF16, 2)
            hr = Ring(es, nc, "hr", [128, T], F32, 2)
            osr = Ring(es, nc, "os", [128, T], F32, 2)
            wl = {}

            def load6(gi):
                w, wb = wring.next()
                t.dma(t.pool, w[:], wov[:, :, gi * 512:(gi + 1) * 512], writes=[wb])
                wl[gi] = (w, wb)
            load6(0)
            for gi in range(4):
                if gi + 1 < 4:
                    load6(gi + 1)
                w, wb = wl.pop(gi)
                for mi in range(4):
                    m = gi * 4 + mi
                    hrt, hrb = hr.next()
                    t.dma(t.sp, hrt[:], hT[m * 128:(m + 1) * 128, HALO:TH], writes=[hrb])
                    os_, osb = osr.next()
                    for tt in range(4):
                        ps, psb = cx.ps()
                        t.op(t.pe, lambda: mm_group(nc, ps, [(w[:, kc, mi * 128:(mi + 1) * 128],
                                                              mg[:, kc, tt * 512:(tt + 1) * 512]) for kc in range(KC)]),
                             reads=[wb, mgb], writes=[psb])
                        t.op(t.dve, lambda: nc.vector.tensor_tensor(out=os_[:, tt * 512:(tt + 1) * 512], in0=ps,
                                                                    in1=hrt[:, tt * 512:(tt + 1) * 512], op=ALU.add),
                             reads=[psb, hrb], writes=[osb])
                    t.dma(t.sp, hmid[m * 128:(m + 1) * 128, :], os_[:], reads=[osb])
            t.barrier()
    return nc


F_G = 0
F_W0 = 16
F_W1 = 16 + 88
F_W2 = 16 + 176
F_B = 16 + 264
F_FING = 16 + 352
NV_FFN = 16 + 352 + 16
FT = 410


def build_ffn(final=False, dbg=False):
    nc = bass.Bass("TRN2", target_bir_lowering=False)
    cx = Ctx(nc)
    t = cx.t

    def din(name, shape):
        return nc.dram_tensor(name, shape, F32, kind="ExternalInput").ap()

    hm = din("hm", [D, T + 2])
    vecs_d = din("vecs", [128, NV_FFN])
    w_up = din("w_up", [D, 2 * D_FF])
    w_down = din("w_down", [D_FF, D])
    out = nc.dram_tensor("out", [D, T], F32, kind="ExternalOutput").ap()
    if dbg:
        act_d = nc.dram_tensor("act_d", [D_FF, T], BF16, kind="ExternalOutput").ap()
    else:
        act_d = nc.dram_tensor("act_d", [D_FF, T], BF16).ap()
    hout_d = nc.dram_tensor("hout_d", [D, T], F32).ap() if final else out

    wup_v = w_up.rearrange("(kc p) n -> p kc n", p=128)
    wdn_v = w_down.rearrange("(kc p) n -> p kc n", p=128)

    with ExitStack() as g:
        vecs = sb(g, nc, "vecs", [128, NV_FFN], F32)
        vb = Buf()
        t.dma(t.sp, vecs[:], vecs_d, writes=[vb])
        cn = load_consts(cx, g, None)
        eps_t, ones_bf = cn["eps"], cn["ones_bf"]
        t.barrier()

        with ExitStack() as es:
            NT = T + 2
            xn = sb(es, nc, "hn", [128, KC, NT], BF16)
            ntile = [(i * FT, FT) for i in range(5)]
            xnb = [Buf() for _ in ntile]
            with ExitStack() as es1:
                rmsnorm_phase(cx, es1, hm, ntile, vecs[:, F_G:F_G + 16], xn, xnb, eps_t, ones_bf)
                t.barrier()
            wring = Ring(es, nc, "wu", [128, 2, KC, 512], BF16, 2)
            tvr = Ring(es, nc, "tv", [128, FT], F32, 3)
            tgr = Ring(es, nc, "tg", [128, FT], F32, 3)
            str_ = Ring(es, nc, "sta", [128, T], BF16, 2)
            wl = {}

            def loadu(gi):
                w, wb = wring.next()
                t.dma(t.pool, w[:, 0, :, :], wup_v[:, :, gi * 512:(gi + 1) * 512], writes=[wb])
                t.dma(t.pool, w[:, 1, :, :], wup_v[:, :, D_FF + gi * 512: D_FF + (gi + 1) * 512], writes=[wb])
                wl[gi] = (w, wb)
            loadu(0)
            for gi in range(11):
                if gi + 1 < 11:
                    loadu(gi + 1)
                w, wb = wl.pop(gi)
                for mi in range(4):
                    c = gi * 4 + mi
                    st, stb = str_.next()
                    for tt in range(5):
                        o0 = tt * FT
                        no = min(FT, T - o0)
                        n = no + 2
                        res = []
                        for half, ring in ((0, tvr), (1, tgr)):
                            ps, psb = cx.ps()
                            t.op(t.pe, lambda: mm_group(nc, ps[:, :n], [(w[:, half, kc, mi * 128:(mi + 1) * 128],
                                                                         xn[:, kc, o0:o0 + n]) for kc in range(KC)]),
                                 reads=[wb, xnb[tt]], writes=[psb])
                            cc = half * NFC + c
                            tv, tvb = ring.next()
                            t.op(t.act, lambda: nc.scalar.activation(out=tv[:, :no], in_=ps[:, 2:2 + no], func=AF.Identity,
                                                                     bias=vecs[:, F_B + cc:F_B + cc + 1],
                                                                     scale=vecs[:, F_W2 + cc:F_W2 + cc + 1]),
                                 reads=[psb, vb], writes=[tvb])
                            t.op(t.dve, lambda: nc.vector.scalar_tensor_tensor(out=tv[:, :no], in0=ps[:, 1:1 + no],
                                                                               scalar=vecs[:, F_W1 + cc:F_W1 + cc + 1],
                                                                               in1=tv[:, :no], op0=ALU.mult, op1=ALU.add),
                                 reads=[psb, tvb, vb], writes=[tvb])
                            t.op(t.dve, lambda: nc.vector.scalar_tensor_tensor(out=tv[:, :no], in0=ps[:, 0:no],
                                                                               scalar=vecs[:, F_W0 + cc:F_W0 + cc + 1],
                                                                               in1=tv[:, :no], op0=ALU.mult, op1=ALU.add),
                                 reads=[psb, tvb, vb], writes=[tvb])
                            res.append((tv, tvb))
                        (tv, tvb), (tg, tgb) = res
                        t.op(t.act, lambda: nc.scalar.activation(out=tg[:, :no], in_=tg[:, :no], func=AF.Silu),
                             reads=[tgb], writes=[tgb])
                        t.op(t.dve, lambda: nc.vector.tensor_tensor(out=st[:, o0:o0 + no], in0=tg[:, :no], in1=tv[:, :no],
                                                                    op=ALU.mult),
                             reads=[tgb, tvb], writes=[stb])
                    t.dma(t.sp, act_d[c * 128:(c + 1) * 128, :], st[:], reads=[stb])
            t.barrier()

        with ExitStack() as es:
            wring = Ring(es, nc, "wd", [128, NFC, 256], BF16, 2)
            hr = Ring(es, nc, "hr", [128, 1024], F32, 2)
            osr = Ring(es, nc, "os", [128, 1024], F32, 2)
            av = act_d.rearrange("(c p) t -> p c t", p=128)
            for hf in range(2):
                with ExitStack() as es2:
                    at = sb(es2, nc, "at", [128, NFC, 1024], BF16)
                    atb = Buf()
                    for q4 in range(4):
                        t.dma(t.sp, at[:, q4 * 11:(q4 + 1) * 11, :], av[:, q4 * 11:(q4 + 1) * 11, hf * 1024:(hf + 1) * 1024],
                              writes=[atb])
                    wl = {}

                    def loadd(gi):
                        w, wb = wring.next()
                        t.dma(t.pool, w[:], wdn_v[:, :, gi * 256:(gi + 1) * 256], writes=[wb])
                        wl[gi] = (w, wb)
                    loadd(0)
                    for gi in range(8):
                        if gi + 1 < 8:
                            loadd(gi + 1)
                        w, wb = wl.pop(gi)
                        for mi in range(2):
                            m = gi * 2 + mi
                            hrt, hrb = hr.next()
                            t.dma(t.sp, hrt[:], hm[m * 128:(m + 1) * 128, 2 + hf * 1024: 2 + (hf + 1) * 1024], writes=[hrb])
                            os_, osb = osr.next()
                            for tt in range(2):
                                ps, psb = cx.ps()
                                t.op(t.pe, lambda: mm_group(nc, ps, [(w[:, kc, mi * 128:(mi + 1) * 128],
                                                                      at[:, kc, tt * 512:(tt + 1) * 512]) for kc in range(NFC)]),
                                     reads=[wb, atb], writes=[psb])
                                t.op(t.dve, lambda: nc.vector.tensor_tensor(out=os_[:, tt * 512:(tt + 1) * 512], in0=ps,
                                                                            in1=hrt[:, tt * 512:(tt + 1) * 512], op=ALU.add),
                                     reads=[psb, hrb], writes=[osb])
                            t.dma(t.sp, hout_d[m * 128:(m + 1) * 128, hf * 1024:(hf + 1) * 1024], os_[:], reads=[osb])
                    t.barrier()

        if final:
            with ExitStack() as es:
                gv = vecs[:, F_FING:F_FING + 16]
                src = hout_d.rearrange("(kc p) t -> p kc t", p=128)
                dst = out.rearrange("(kc p) t -> p kc t", p=128)
                hring = Ring(es, nc, "fh", [128, KC, 512], F32, 2)
                sqring = Ring(es, nc, "fsq", [128, KC, 512], BF16, 2)
                sdring = Ring(es, nc, "fsd", [128, 512], F32, 2)
                rsring = Ring(es, nc, "frs", [128, 512], F32, 2)
                oring = Ring(es, nc, "fo", [128, KC, 512], F32, 2)
                for ti in range(4):
                    j0 = ti * 512
                    hb, hbb = hring.next()
                    t.dma(t.sp, hb[:], src[:, :, j0:j0 + 512], writes=[hbb])
                    sq, sqb = sqring.next()
                    t.op(t.act, lambda: nc.scalar.activation(out=sq[:], in_=hb[:], func=AF.Square), reads=[hbb], writes=[sqb])
                    ps, psb = cx.ps()
                    t.op(t.pe, lambda: mm_group(nc, ps, [(ones_bf[:, :], sq[:, kc, :]) for kc in range(KC)]),
                         reads=[sqb, cn["b"]], writes=[psb])
                    sd, sdb = sdring.next()
                    t.op(t.act, lambda: nc.scalar.activation(out=sd[:], in_=ps, func=AF.Sqrt, bias=eps_t[:, 0:1], scale=1.0 / D),
                         reads=[psb], writes=[sdb])
                    rs, rsb = rsring.next()
                    t.op(t.dve, lambda: nc.vector.reciprocal(rs[:], sd[:]), reads=[sdb], writes=[rsb])
                    o, ob = oring.next()

                    def body():
                        ins = None
                        for kc in range(KC):
                            ins = nc.vector.scalar_tensor_tensor(out=o[:, kc, :], in0=hb[:, kc, :], scalar=gv[:, kc:kc + 1],
                                                                 in1=rs[:], op0=ALU.mult, op1=ALU.mult)
                        return ins
                    t.op(t.dve, body, reads=[hbb, rsb, vb], writes=[ob])
                    t.dma(t.sp, dst[:, :, j0:j0 + 512], o[:], reads=[ob])
                t.barrier()
    return nc


def _cols(v, nch):
    return np.ascontiguousarray(np.asarray(v, np.float32).reshape(nch, 128).T)


def mixer_vecs(inp, l):
    v = np.zeros((128, NV_MIX), np.float32)
    v[:, V_MIXG:V_MIXG + 16] = _cols(inp["mix_norm_g"][l], 16)
    v[:, V_GATEB:V_GATEB + 48] = _cols(inp["gate_b"][l], 48)
    v[:, V_CONVB:V_CONVB + 8] = _cols(inp["conv_b"][l], 8)
    v[:, V_LNG:V_LNG + 8] = _cols(inp["conv_ln_g"][l], 8)
    v[:, V_LNB:V_LNB + 8] = _cols(inp["conv_ln_b"][l], 8)
    cw = np.asarray(inp["conv_w"][l], np.float32)
    cw = cw.reshape(31, 8, 128).transpose(2, 1, 0)
    v[:, V_CONVW:V_CONVW + 248] = cw.reshape(128, 248)
    v[:, V_MEMG:V_MEMG + 16] = _cols(inp["mem_norm_g"][l], 16)
    return v


def ffn_vecs(inp, l):
    v = np.zeros((128, NV_FFN), np.float32)
    v[:, F_G:F_G + 16] = _cols(inp["ffn_norm_g"][l], 16)
    fw = np.asarray(inp["ffn_conv_w"][l], np.float32)
    v[:, F_W0:F_W0 + 88] = _cols(fw[0], 88)
    v[:, F_W1:F_W1 + 88] = _cols(fw[1], 88)
    v[:, F_W2:F_W2 + 88] = _cols(fw[2], 88)
    v[:, F_B:F_B + 88] = _cols(inp["ffn_conv_b"][l], 88)
    v[:, F_FING:F_FING + 16] = _cols(inp["final_norm_g"], 16)
    return v


def bias_blocks(rel_bias_l):
    rb = np.asarray(rel_bias_l, np.float32)
    kl = np.arange(128)[:, None]
    ql = np.arange(128)[None, :]
    out = np.empty((16, 128, 640), np.float32)
    for dd in range(5):
        dist = 128 * (4 - dd) + ql - kl
        idx = np.clip(dist, -256, 256) + 256
        blk = rb[:, idx]
        if dd == 0:
            mask = (kl < 64) & (ql >= 64)
            blk = np.where(mask[None], np.float32(NEG), blk)
        if dd == 4:
            mask = (kl >= 64) & (ql < 64)
            blk = np.where(mask[None], np.float32(NEG), blk)
        out[:, :, dd * 128:(dd + 1) * 128] = blk
    return out


_progs = {}


def _prog(name):
    if name not in _progs:
        if name == "mixer":
            _progs[name] = build_mixer()
        elif name == "ffn":
            _progs[name] = build_ffn(final=False)
        else:
            _progs[name] = build_ffn(final=True)
    return _progs[name]


def _with_halo(hT_full, halo):
    outs = []
    B = hT_full.shape[0]
    for b in range(B):
        for s in range(4):
            t0 = s * T
            a = np.zeros((D, halo + T), np.float32)
            a[:, halo:] = hT_full[b, :, t0:t0 + T]
            if s > 0:
                a[:, :halo] = hT_full[b, :, t0 - halo:t0]
            outs.append(a)
    return outs


def kernel(**inp):
    inp = {k: np.asarray(v) for k, v in inp.items()}
    x = inp["x"].astype(np.float32, copy=False)
    B, S, _ = x.shape
    hT = np.ascontiguousarray(x.transpose(0, 2, 1))
    memT = np.ascontiguousarray(inp["mem"].transpose(0, 2, 1))
    ident = np.eye(128, dtype=np.float32)
    cores = list(range(NCORES))
    for l in range(DEPTH):
        hs = _with_halo(hT, HALO)
        mv = mixer_vecs(inp, l)
        bb = bias_blocks(inp["rel_bias"][l])
        in_maps = []
        for c in cores:
            b, s = divmod(c, 4)
            in_maps.append({
                "hT": hs[c], "memT": memT[b], "vecs": mv,
                "w_in": inp["w_in"][l], "w_conv_out": inp["w_conv_out"][l], "w_attn_out": inp["w_attn_out"][l],
                "w_mem_kv": inp["w_mem_kv"][l], "w_mem_out": inp["w_mem_out"][l], "w_o": inp["w_o"][l],
                "biasblk": bb,
                "halo_ones": np.full((128, 64), 0.0 if s == 0 else 1.0, np.float32),
                "ident": ident,
            })
        res = run_bass_kernel_spmd(_prog("mixer"), in_maps, core_ids=cores)
        hmT = np.stack([np.concatenate([res.results[b * 4 + s]["hmid"] for s in range(4)], axis=1) for b in range(B)])
        hs2 = _with_halo(hmT, 2)
        fv = ffn_vecs(inp, l)
        in_maps = [{"hm": hs2[c], "vecs": fv, "w_up": inp["w_up"][l], "w_down": inp["w_down"][l]} for c in cores]
        res = run_bass_kernel_spmd(_prog("ffn_final" if l == DEPTH - 1 else "ffn"), in_maps, core_ids=cores)
        hT = np.stack([np.concatenate([res.results[b * 4 + s]["out"] for s in range(4)], axis=1) for b in range(B)])
    return np.ascontiguousarray(hT.transpose(0, 2, 1)).astype(np.float32)
```

```python
from contextlib import ExitStack
import numpy as np
import concourse.bass as bass
import concourse.mybir as mybir
from concourse.bass_utils import run_bass_kernel_spmd

F32 = mybir.dt.float32
BF16 = mybir.dt.bfloat16
AF = mybir.ActivationFunctionType
ALU = mybir.AluOpType

D = 2048
KC = 16
T = 2048
HALO = 512
TH = T + HALO
DEPTH = 4
NCORES = 8
D_FF = 5632
NFC = 44
EPS = 1e-6
NEG = -30000.0


class Buf:
    __slots__ = ("last_w", "readers")

    def __init__(self):
        self.last_w = None
        self.readers = {}


class Eng:
    def __init__(self, nc, name, handle):
        self.name = name
        self.h = handle
        self.sem = nc.alloc_semaphore("s_" + name)
        self.count = 0
        self.seen = {}


class Trk:
    def __init__(self, nc, n_dma_sems=32):
        self.nc = nc
        self.pe = Eng(nc, "pe", nc.tensor)
        self.act = Eng(nc, "act", nc.scalar)
        self.dve = Eng(nc, "dve", nc.vector)
        self.pool = Eng(nc, "pool", nc.gpsimd)
        self.sp = Eng(nc, "sp", nc.sync)
        self.engs = [self.pe, self.act, self.dve, self.pool, self.sp]
        self.dsems = [[nc.alloc_semaphore(f"d{i}"), 0] for i in range(n_dma_sems)]
        self.dnext = 0

    def _wait(self, eng, toks):
        best = {}
        for (sem, val, key) in toks:
            if key not in best or best[key][1] < val:
                best[key] = (sem, val)
        for key, (sem, val) in best.items():
            if eng.seen.get(key, 0) >= val:
                continue
            eng.h.wait_ge(sem, val)
            eng.seen[key] = val

    def _deps(self, eng, reads, writes):
        toks = []
        for b in reads:
            if b.last_w is not None:
                if not (b.last_w[2] == eng.name and eng.name == "pe"):
                    toks.append(b.last_w)
        for b in writes:
            if b.last_w is not None and b.last_w[2] != eng.name:
                toks.append(b.last_w)
            for k, tk in b.readers.items():
                if k != eng.name:
                    toks.append(tk)
        return toks

    def op(self, eng, fn, reads=(), writes=()):
        self._wait(eng, self._deps(eng, reads, writes))
        ins = fn()
        eng.count += 1
        ins.then_inc(eng.sem, 1)
        tok = (eng.sem, eng.count, eng.name)
        for b in reads:
            b.readers[eng.name] = tok
        for b in writes:
            b.last_w = tok
            b.readers = {}
        return ins

    def dma(self, eng, out, in_, reads=(), writes=(), **kw):
        toks = self._deps(eng, reads, writes)
        slot = self.dsems[self.dnext]
        key = f"dma{self.dnext}"
        self.dnext = (self.dnext + 1) % len(self.dsems)
        if slot[1] > 0:
            toks.append((slot[0], slot[1], key))
        self._wait(eng, toks)
        ins = eng.h.dma_start(out=out, in_=in_, **kw)
        slot[1] += 16
        ins.then_inc(slot[0], 16)
        tok = (slot[0], slot[1], key)
        for b in reads:
            b.readers[key] = tok
        for b in writes:
            b.last_w = tok
            b.readers = {}
        return ins

    def barrier(self):
        toks = [(e.sem, e.count, e.name) for e in self.engs if e.count > 0]
        toks += [(s[0], s[1], f"dma{i}") for i, s in enumerate(self.dsems) if s[1] > 0]
        for e in self.engs:
            self._wait(e, [tk for tk in toks if tk[2] != e.name])


_uid = [0]


def uid():
    _uid[0] += 1
    return _uid[0]


class Ring:
    def __init__(self, es, nc, name, shape, dtype, n):
        self.t = [es.enter_context(nc.sbuf_tensor(f"{name}_{i}_{uid()}", shape, dtype)) for i in range(n)]
        self.b = [Buf() for _ in range(n)]
        self.i = 0

    def next(self):
        k = self.i % len(self.t)
        self.i += 1
        return self.t[k], self.b[k]


class Ctx:
    def __init__(self, nc):
        self.nc = nc
        self.t = Trk(nc)
        self.psall = nc.alloc_psum_tensor("psall", [128, 8, 512], F32)
        self.psb = [Buf() for _ in range(8)]
        self.psi = 0
        self.ev = 0

    def ps(self):
        k = self.psi % 8
        self.psi += 1
        return self.psall[:, k, :], self.psb[k]

    def ps_pair(self):
        if self.psi % 2:
            self.psi += 1
        k = self.psi % 8
        self.psi += 2
        flat = self.psall[:, k:k + 2, :].rearrange("p a b -> p (a b)")
        return flat, [self.psb[k], self.psb[k + 1]]


def mm_group(nc, out, pairs):
    n = len(pairs)
    ins = None
    for i, (l, r) in enumerate(pairs):
        ins = nc.tensor.matmul(out, l, r, start=(i == 0), stop=(i == n - 1))
    return ins


def sb(es, nc, name, shape, dtype):
    return es.enter_context(nc.sbuf_tensor(f"{name}_{uid()}", shape, dtype))


def rmsnorm_phase(cx, es, srcT, tiles, gvec, xn, xnb, eps_t, ones_bf):
    nc, t = cx.nc, cx.t
    src = srcT.rearrange("(kc p) t -> p kc t", p=128)
    hring = Ring(es, nc, "nh", [128, KC, 512], F32, 2)
    sqring = Ring(es, nc, "nsq", [128, KC, 512], BF16, 2)
    sdring = Ring(es, nc, "nsd", [128, 512], F32, 2)
    rsring = Ring(es, nc, "nrs", [128, 512], F32, 2)
    for ti, (j0, n) in enumerate(tiles):
        hb, hbb = hring.next()
        t.dma(t.sp, hb[:, :, :n], src[:, :, j0:j0 + n], writes=[hbb])
        sq, sqb = sqring.next()
        t.op(t.act, lambda: nc.scalar.activation(out=sq[:, :, :n], in_=hb[:, :, :n], func=AF.Square),
             reads=[hbb], writes=[sqb])
        ps, psb = cx.ps()
        t.op(t.pe, lambda: mm_group(nc, ps[:, :n], [(ones_bf[:, :], sq[:, kc, :n]) for kc in range(KC)]),
             reads=[sqb], writes=[psb])
        sd, sdb = sdring.next()
        t.op(t.act, lambda: nc.scalar.activation(out=sd[:, :n], in_=ps[:, :n], func=AF.Sqrt,
                                                 bias=eps_t[:, 0:1], scale=1.0 / D),
             reads=[psb], writes=[sdb])
        rs, rsb = rsring.next()
        t.op(t.dve, lambda: nc.vector.reciprocal(rs[:, :n], sd[:, :n]), reads=[sdb], writes=[rsb])

        def body():
            ins = None
            for kc in range(KC):
                ins = nc.vector.scalar_tensor_tensor(out=xn[:, kc, j0:j0 + n], in0=hb[:, kc, :n],
                                                     scalar=gvec[:, kc:kc + 1], in1=rs[:, :n],
                                                     op0=ALU.mult, op1=ALU.mult)
            return ins
        t.op(t.dve, body, reads=[hbb, rsb], writes=[xnb[ti]])


def load_consts(cx, es, ident_d):
    nc, t = cx.nc, cx.t
    c = {}
    c["eps"] = sb(es, nc, "eps", [128, 1], F32)
    c["ones_bf"] = sb(es, nc, "ones_bf", [128, 128], BF16)
    c["b"] = Buf()
    t.op(t.dve, lambda: nc.vector.memset(c["eps"][:], EPS), writes=[c["b"]])
    t.op(t.dve, lambda: nc.vector.memset(c["ones_bf"][:], 1.0), writes=[c["b"]])
    return c


V_MIXG = 0
V_GATEB = 16
V_CONVB = 64
V_LNG = 72
V_LNB = 80
V_CONVW = 88
V_MEMG = 88 + 248
NV_MIX = 352


def build_mixer(dbg=False, stop=99):
    nc = bass.Bass("TRN2", target_bir_lowering=False)
    cx = Ctx(nc)
    t = cx.t

    def din(name, shape):
        return nc.dram_tensor(name, shape, F32, kind="ExternalInput").ap()

    def dscr(name, shape, dt):
        if dbg:
            return nc.dram_tensor(name, shape, dt, kind="ExternalOutput").ap()
        return nc.dram_tensor(name, shape, dt).ap()

    hT = din("hT", [D, TH])
    memT = din("memT", [D, 256])
    vecs_d = din("vecs", [128, NV_MIX])
    w_in = din("w_in", [D, 12288])
    w_conv_out = din("w_conv_out", [1024, D])
    w_attn_out = din("w_attn_out", [1024, D])
    w_mem_kv = din("w_mem_kv", [D, 2048])
    w_mem_out = din("w_mem_out", [1024, D])
    w_o = din("w_o", [D, D])
    biasblk = din("biasblk", [16, 128, 640])
    halo_ones_d = din("halo_ones", [128, 64])
    ident_d = din("ident", [128, 128])
    hmid = nc.dram_tensor("hmid", [D, T], F32, kind="ExternalOutput").ap()

    qT_d = dscr("qT_d", [1024, T], BF16)
    kT_d = dscr("kT_d", [1024, TH], BF16)
    V_d = dscr("V_d", [TH, 1024], BF16)
    qm_d = dscr("qm_d", [1024, T], BF16)
    gates_d = dscr("gates_d", [6144, T], BF16)
    hglu_d = dscr("hglu_d", [1024, 2080], BF16)
    cact_d = dscr("cact_d", [1024, T], BF16)
    attn_d = dscr("attn_d", [1024, T], BF16)
    memo_d = dscr("memo_d", [1024, T], BF16)
    merged_d = dscr("merged_d", [D, T], BF16)

    win_v = w_in.rearrange("(kc p) n -> p kc n", p=128)

    with ExitStack() as g:
        vecs = sb(g, nc, "vecs", [128, NV_MIX], F32)
        vb = Buf()
        t.dma(t.sp, vecs[:], vecs_d, writes=[vb])
        cn = load_consts(cx, g, ident_d)
        eps_t, ones_bf = cn["eps"], cn["ones_bf"]
        ident_bf = sb(g, nc, "ident_bf", [128, 128], BF16)
        ident_f = sb(g, nc, "ident_f", [128, 128], F32)
        t.dma(t.pool, ident_bf[:], ident_d, writes=[cn["b"]])
        t.dma(t.sp, ident_f[:], ident_d, writes=[cn["b"]])
        t.barrier()
        if stop <= 0:
            return nc

        with ExitStack() as es:
            xn = sb(es, nc, "xn", [128, KC, TH], BF16)
            ntile = [(i * 512, 512) for i in range(5)]
            xnb = [Buf() for _ in ntile]
            with ExitStack() as es1:
                rmsnorm_phase(cx, es1, hT, ntile, vecs[:, V_MIXG:V_MIXG + 16], xn, xnb, eps_t, ones_bf)
                t.barrier()
                if stop <= 1:
                    return nc

            def xbufs(j0, n):
                return [xnb[i] for i in range(5) if i * 512 < j0 + n and (i + 1) * 512 > j0]

            wring = Ring(es, nc, "w", [128, KC, 512], BF16, 3)
            stq = Ring(es, nc, "stq", [128, TH], BF16, 2)
            stv = Ring(es, nc, "stv", [128, 512], BF16, 3)
            sgr = Ring(es, nc, "sg", [128, 416], F32, 2)

            groups = [("glu", i) for i in range(4)] + [("q", i) for i in range(2)] + \
                     [("k", i) for i in range(2)] + [("v", i) for i in range(2)] + \
                     [("qm", i) for i in range(2)] + [("gate", i) for i in range(12)]
            loaded = {}

            def load(gi):
                kind, i = groups[gi]
                w, wb = wring.next()
                if kind == "glu":
                    t.dma(t.pool, w[:, :, 0:256], win_v[:, :, i * 256:(i + 1) * 256], writes=[wb])
                    t.dma(t.pool, w[:, :, 256:512], win_v[:, :, 1024 + i * 256:1024 + (i + 1) * 256], writes=[wb])
                else:
                    base = {"q": 2048, "k": 3072, "v": 4096, "qm": 5120, "gate": 6144}[kind]
                    t.dma(t.pool, w[:], win_v[:, :, base + i * 512: base + (i + 1) * 512], writes=[wb])
                loaded[gi] = (w, wb)

            def evac_copy(dst, ps, psb, dstb, scale=None):
                cx.ev += 1
                if cx.ev % 2 == 0:
                    if scale is None:
                        t.op(t.act, lambda: nc.scalar.activation(out=dst, in_=ps, func=AF.Copy),
                             reads=[psb], writes=[dstb])
                    else:
                        t.op(t.act, lambda: nc.scalar.activation(out=dst, in_=ps, func=AF.Copy, scale=scale),
                             reads=[psb], writes=[dstb])
                else:
                    if scale is None:
                        t.op(t.dve, lambda: nc.vector.tensor_copy(dst, ps), reads=[psb], writes=[dstb])
                    else:
                        t.op(t.dve, lambda: nc.vector.tensor_scalar(out=dst, in0=ps, scalar1=scale, scalar2=None,
                                                                    op0=ALU.mult),
                             reads=[psb], writes=[dstb])

            load(0)
            load(1)
            for gi, (kind, i) in enumerate(groups):
                if gi + 2 < len(groups):
                    load(gi + 2)
                w, wb = loaded.pop(gi)
                if kind in ("q", "k", "qm", "gate"):
                    j_lo = 0 if kind == "k" else HALO
                    ntt = 5 if kind == "k" else 4
                    for mi in range(4):
                        m = i * 4 + mi
                        st, stb = stq.next()
                        for tt in range(ntt):
                            j0 = j_lo + tt * 512
                            ps, psb = cx.ps()
                            t.op(t.pe, lambda: mm_group(nc, ps, [(w[:, kc, mi * 128:(mi + 1) * 128],
                                                                  xn[:, kc, j0:j0 + 512]) for kc in range(KC)]),
                                 reads=[wb] + xbufs(j0, 512), writes=[psb])
                            dst = st[:, tt * 512:(tt + 1) * 512]
                            if kind == "gate":
                                t.op(t.act, lambda: nc.scalar.activation(
                                    out=dst, in_=ps, func=AF.Sigmoid,
                                    bias=vecs[:, V_GATEB + m:V_GATEB + m + 1], scale=1.0),
                                    reads=[psb, vb], writes=[stb])
                            elif kind == "q":
                                evac_copy(dst, ps, psb, stb, scale=0.125)
                            else:
                                evac_copy(dst, ps, psb, stb)
                        dd = {"q": qT_d, "k": kT_d, "qm": qm_d, "gate": gates_d}[kind]
                        t.dma(t.sp, dd[m * 128:(m + 1) * 128, :], st[:, :ntt * 512], reads=[stb])
                elif kind == "v":
                    for tk in range(20):
                        ps, psb = cx.ps()
                        t.op(t.pe, lambda: mm_group(nc, ps, [(xn[:, kc, tk * 128:(tk + 1) * 128], w[:, kc, :])
                                                              for kc in range(KC)]),
                             reads=[wb] + xbufs(tk * 128, 128), writes=[psb])
                        st, stb = stv.next()
                        evac_copy(st[:], ps, psb, stb)
                        t.dma(t.sp, V_d[tk * 128:(tk + 1) * 128, i * 512:(i + 1) * 512], st[:], reads=[stb])
                else:
                    for ci in range(2):
                        c = 2 * i + ci
                        st, stb = stq.next()
                        for tt in range(5):
                            j0 = 480 + tt * 416
                            psa, psab = cx.ps()
                            psg, psgb = cx.ps()
                            t.op(t.pe, lambda: mm_group(nc, psa[:, :416], [(w[:, kc, ci * 128:(ci + 1) * 128],
                                                                            xn[:, kc, j0:j0 + 416]) for kc in range(KC)]),
                                 reads=[wb] + xbufs(j0, 416), writes=[psab])
                            t.op(t.pe, lambda: mm_group(nc, psg[:, :416], [(w[:, kc, 256 + ci * 128:256 + (ci + 1) * 128],
                                                                            xn[:, kc, j0:j0 + 416]) for kc in range(KC)]),
                                 reads=[wb] + xbufs(j0, 416), writes=[psgb])
                            sg, sgb = sgr.next()
                            t.op(t.act, lambda: nc.scalar.activation(out=sg[:], in_=psg[:, :416], func=AF.Sigmoid),
                                 reads=[psgb], writes=[sgb])
                            t.op(t.dve, lambda: nc.vector.tensor_tensor(out=st[:, tt * 416:(tt + 1) * 416],
                                                                        in0=psa[:, :416], in1=sg[:], op=ALU.mult),
                                 reads=[psab, sgb], writes=[stb])
                        t.dma(t.sp, hglu_d[c * 128:(c + 1) * 128, :], st[:, :2080], reads=[stb])
            t.barrier()

        if stop <= 2:
            return nc
        with ExitStack() as es:
            hg = sb(es, nc, "hg", [128, 8, 2080], BF16)
            hgb = [Buf() for _ in range(8)]
            hv = hglu_d.rearrange("(c p) t -> p c t", p=128)
            for c in range(8):
                t.dma(t.sp, hg[:, c, :], hv[:, c, :], writes=[hgb[c]])
            dg = sb(es, nc, "dg", [128, 8, 31, 128], BF16)
            dgb = [Buf() for _ in range(8)]
            for c in range(8):
                def mk():
                    ins = None
                    for k in range(31):
                        ins = nc.vector.tensor_scalar(out=dg[:, c, k, :], in0=ident_f[:, :],
                                                      scalar1=vecs[:, V_CONVW + c * 31 + k:V_CONVW + c * 31 + k + 1],
                                                      scalar2=None, op0=ALU.mult)
                    return ins
                t.op(t.dve, mk, reads=[vb], writes=[dgb[c]])
            onesS = sb(es, nc, "onesS", [128, 128], BF16)
            osb = Buf()
            t.op(t.dve, lambda: nc.vector.memset(onesS[:], 1.0 / 1024.0), writes=[osb])
            yr = Ring(es, nc, "y", [128, 8, 512], F32, 2)
            ybr = Ring(es, nc, "yb", [128, 8, 512], BF16, 2)
            ysr = Ring(es, nc, "ys", [128, 8, 512], BF16, 2)
            smr = Ring(es, nc, "sm", [128, 512], F32, 2)
            sqr_ = Ring(es, nc, "sq", [128, 512], F32, 2)
            srr = Ring(es, nc, "sr", [128, 512], F32, 2)
            tmr = Ring(es, nc, "tm", [128, 512], F32, 3)
            cst = Ring(es, nc, "cst", [128, 512], BF16, 3)
            for tt in range(4):
                y, yb = yr.next()
                ybf, ybfb = ybr.next()
                ysq, ysqb = ysr.next()
                for c in range(8):
                    ps, psb = cx.ps()
                    t.op(t.pe, lambda: mm_group(nc, ps, [(dg[:, c, k, :], hg[:, c, tt * 512 + 2 + k: tt * 512 + 2 + k + 512])
                                                          for k in range(31)]),
                         reads=[dgb[c], hgb[c]], writes=[psb])
                    cb = vecs[:, V_CONVB + c:V_CONVB + c + 1]
                    t.op(t.act, lambda: nc.scalar.activation(out=y[:, c, :], in_=ps, func=AF.Identity, bias=cb, scale=1.0),
                         reads=[psb, vb], writes=[yb])
                    t.op(t.dve, lambda: nc.vector.tensor_copy(ybf[:, c, :], y[:, c, :]),
                         reads=[yb], writes=[ybfb])
                    t.op(t.act, lambda: nc.scalar.activation(out=ysq[:, c, :], in_=ps, func=AF.Square, bias=cb, scale=1.0),
                         reads=[psb, vb], writes=[ysqb])
                pm, pmb = cx.ps()
                t.op(t.pe, lambda: mm_group(nc, pm, [(onesS[:, :], ybf[:, c, :]) for c in range(8)]),
                     reads=[osb, ybfb], writes=[pmb])
                pe2, pe2b = cx.ps()
                t.op(t.pe, lambda: mm_group(nc, pe2, [(onesS[:, :], ysq[:, c, :]) for c in range(8)]),
                     reads=[osb, ysqb], writes=[pe2b])
                sm, smb = smr.next()
                sq, sqb = sqr_.next()
                sr, srb = srr.next()
                t.op(t.act, lambda: nc.scalar.activation(out=sm[:], in_=pm, func=AF.Copy), reads=[pmb], writes=[smb])
                t.op(t.act, lambda: nc.scalar.activation(out=sq[:], in_=pm, func=AF.Square), reads=[pmb], writes=[sqb])
                t.op(t.dve, lambda: nc.vector.tensor_tensor(out=sq[:], in0=pe2, in1=sq[:], op=ALU.subtract),
                     reads=[pe2b, sqb], writes=[sqb])
                t.op(t.act, lambda: nc.scalar.activation(out=sq[:], in_=sq[:], func=AF.Sqrt, bias=eps_t[:, 0:1], scale=1.0),
                     reads=[sqb], writes=[sqb])
                t.op(t.dve, lambda: nc.vector.reciprocal(sr[:], sq[:]), reads=[sqb], writes=[srb])
                for c in range(8):
                    tm, tmb = tmr.next()
                    t.op(t.dve, lambda: nc.vector.tensor_tensor(out=tm[:], in0=y[:, c, :], in1=sm[:], op=ALU.subtract),
                         reads=[yb, smb], writes=[tmb])
                    t.op(t.dve, lambda: nc.vector.tensor_tensor(out=tm[:], in0=tm[:], in1=sr[:], op=ALU.mult),
                         reads=[tmb, srb], writes=[tmb])
                    cs, csb = cst.next()
                    t.op(t.act, lambda: nc.scalar.activation(out=cs[:], in_=tm[:], func=AF.Silu,
                                                             bias=vecs[:, V_LNB + c:V_LNB + c + 1],
                                                             scale=vecs[:, V_LNG + c:V_LNG + c + 1]),
                         reads=[tmb, vb], writes=[csb])
                    t.dma(t.sp, cact_d[c * 128:(c + 1) * 128, tt * 512:(tt + 1) * 512], cs[:], reads=[csb])
            t.barrier()

        if stop <= 3:
            return nc
        with ExitStack() as es:
            ho = sb(es, nc, "halo_ones", [128, 64], BF16)
            hob = Buf()
            t.dma(t.pool, ho[:], halo_ones_d, writes=[hob])
            kv_ = kT_d.rearrange("(c p) t -> p c t", p=128)
            qv_ = qT_d.rearrange("(c p) t -> p c t", p=128)
            vv_ = V_d.rearrange("(k p) c -> p k c", p=128)
            bv_ = biasblk.rearrange("h p c -> p h c")
            ptr = Ring(es, nc, "pt", [128, 640], BF16, 3)
            rcr = Ring(es, nc, "rc", [128, 128], F32, 3)
            aor = Ring(es, nc, "ao", [128, T], BF16, 2)
            for hh in range(2):
                with ExitStack() as es2:
                    kT = sb(es2, nc, "kT", [128, 4, TH], BF16)
                    qT = sb(es2, nc, "qT", [128, 4, T], BF16)
                    Vt = sb(es2, nc, "Vt", [128, 20, 512], BF16)
                    bb = sb(es2, nc, "bb", [128, 8, 640], BF16)
                    lbs = [[Buf(), Buf(), Buf(), Buf()] for _ in range(4)]
                    for c_ in range(4):
                        t.dma(t.sp, kT[:, c_, :], kv_[:, hh * 4 + c_, :], writes=[lbs[c_][0]])
                        t.dma(t.sp, qT[:, c_, :], qv_[:, hh * 4 + c_, :], writes=[lbs[c_][1]])
                        t.dma(t.sp, Vt[:, :, c_ * 128:(c_ + 1) * 128],
                              vv_[:, :, hh * 512 + c_ * 128: hh * 512 + (c_ + 1) * 128], writes=[lbs[c_][2]])
                        t.dma(t.pool, bb[:, 2 * c_:2 * c_ + 2, :], bv_[:, hh * 8 + 2 * c_: hh * 8 + 2 * c_ + 2, :],
                              writes=[lbs[c_][3]])
                    for cc in range(4):
                        lb_k, lb_q, lb_v, lb_b = lbs[cc]
                        ao, aob = aor.next()
                        items = [(qt, hi) for qt in range(16) for hi in range(2)]

                        def emit_qk(qt, hi):
                            hl = 2 * cc + hi
                            pb = 64 * hi
                            S, Sb = cx.ps_pair()

                            def qk():
                                nc.tensor.matmul(S[:, 0:512], ident_bf[:, :], bb[:, hl, 0:512], start=True, stop=False)
                                nc.tensor.matmul(S[:, 512:640], ident_bf[:, :], bb[:, hl, 512:640], start=True, stop=False)
                                ins = None
                                for dd in range(5):
                                    kt = qt + dd
                                    ins = nc.tensor.matmul(S[:, dd * 128:(dd + 1) * 128],
                                                           kT[pb:pb + 64, cc, kt * 128:(kt + 1) * 128],
                                                           qT[pb:pb + 64, cc, qt * 128:(qt + 1) * 128],
                                                           start=False, stop=True)
                                return ins
                            t.op(t.pe, qk, reads=[lb_k, lb_q, lb_b, cn["b"]], writes=Sb)
                            return S, Sb

                        nxt = emit_qk(*items[0])
                        for ii, (qt, hi) in enumerate(items):
                            hl = 2 * cc + hi
                            pb = 64 * hi
                            S, Sb = nxt
                            if ii + 1 < len(items):
                                nxt = emit_qk(*items[ii + 1])
                            pt, ptb = ptr.next()
                            t.op(t.act, lambda: nc.scalar.activation(out=pt[:], in_=S[:, 0:640], func=AF.Exp),
                                 reads=Sb, writes=[ptb])
                            N, Nb = cx.ps()
                            num = N[pb:pb + 64, 0:128]
                            den = N[pb:pb + 64, 128:256]

                            def pv():
                                mm_group(nc, num, [(Vt[:, qt + dd, hl * 64:(hl + 1) * 64], pt[:, dd * 128:(dd + 1) * 128])
                                                   for dd in range(5)])
                                return mm_group(nc, den, [((ho[:, :] if qt + dd < 4 else ones_bf[:, 0:64]),
                                                           pt[:, dd * 128:(dd + 1) * 128]) for dd in range(5)])
                            t.op(t.pe, pv, reads=[lb_v, ptb, hob, cn["b"]], writes=[Nb])
                            rc, rcb = rcr.next()
                            t.op(t.dve, lambda: nc.vector.reciprocal(rc[pb:pb + 64, :], den), reads=[Nb], writes=[rcb])
                            t.op(t.dve, lambda: nc.vector.tensor_tensor(out=ao[pb:pb + 64, qt * 128:(qt + 1) * 128],
                                                                        in0=num, in1=rc[pb:pb + 64, :], op=ALU.mult),
                                 reads=[Nb, rcb], writes=[aob])
                        ch = hh * 4 + cc
                        t.dma(t.sp, attn_d[ch * 128:(ch + 1) * 128, :], ao[:], reads=[aob])
                    t.barrier()

        if stop <= 4:
            return nc
        with ExitStack() as es:
            xm = sb(es, nc, "xm", [128, KC, 256], BF16)
            xmb = [Buf()]
            with ExitStack() as es1:
                rmsnorm_phase(cx, es1, memT, [(0, 256)], vecs[:, V_MEMG:V_MEMG + 16], xm, xmb, eps_t, ones_bf)
                t.barrier()
            KmT = sb(es, nc, "KmT", [128, 8, 256], BF16)
            Vm = sb(es, nc, "Vm", [128, 2, 1024], BF16)
            kmb = Buf()
            vmb = Buf()
            wring = Ring(es, nc, "wm", [128, KC, 512], BF16, 2)
            wkv = w_mem_kv.rearrange("(kc p) n -> p kc n", p=128)
            for gi in range(4):
                w, wb = wring.next()
                t.dma(t.pool, w[:], wkv[:, :, gi * 512:(gi + 1) * 512], writes=[wb])
                if gi < 2:
                    for mi in range(4):
                        ps, psb = cx.ps()
                        t.op(t.pe, lambda: mm_group(nc, ps[:, :256], [(w[:, kc, mi * 128:(mi + 1) * 128], xm[:, kc, :])
                                                                       for kc in range(KC)]),
                             reads=[wb, xmb[0]], writes=[psb])
                        t.op(t.dve, lambda: nc.vector.tensor_copy(KmT[:, gi * 4 + mi, :], ps[:, :256]),
                             reads=[psb], writes=[kmb])
                else:
                    for mt in range(2):
                        ps, psb = cx.ps()
                        t.op(t.pe, lambda: mm_group(nc, ps, [(xm[:, kc, mt * 128:(mt + 1) * 128], w[:, kc, :])
                                                              for kc in range(KC)]),
                             reads=[wb, xmb[0]], writes=[psb])
                        t.op(t.dve, lambda: nc.vector.tensor_copy(Vm[:, mt, (gi - 2) * 512:(gi - 1) * 512], ps),
                             reads=[psb], writes=[vmb])
            qm = sb(es, nc, "qm", [128, 8, T], BF16)
            qmb = Buf()
            t.dma(t.sp, qm[:], qm_d.rearrange("(c p) t -> p c t", p=128), writes=[qmb])
            pmr = Ring(es, nc, "pm", [128, 512], BF16, 4)
            rcr = Ring(es, nc, "rcm", [128, 512], F32, 2)
            mor = Ring(es, nc, "mo", [128, T], BF16, 4)
            for h in range(4):
                mo = [mor.next() for _ in range(2)]
                for tq in range(4):
                    pms = []
                    for mt in range(2):
                        S, Sb = cx.ps()
                        t.op(t.pe, lambda: mm_group(nc, S, [(KmT[:, 2 * h + i, mt * 128:(mt + 1) * 128],
                                                             qm[:, 2 * h + i, tq * 512:(tq + 1) * 512]) for i in range(2)]),
                             reads=[kmb, qmb], writes=[Sb])
                        pm, pmb = pmr.next()
                        t.op(t.act, lambda: nc.scalar.activation(out=pm[:], in_=S, func=AF.Exp, scale=1.0 / 16.0),
                             reads=[Sb], writes=[pmb])
                        pms.append((pm, pmb))
                    den, denb = cx.ps()
                    t.op(t.pe, lambda: mm_group(nc, den, [(ones_bf[:, :], pms[mt][0][:]) for mt in range(2)]),
                         reads=[cn["b"], pms[0][1], pms[1][1]], writes=[denb])
                    rc, rcb = rcr.next()
                    t.op(t.dve, lambda: nc.vector.reciprocal(rc[:], den), reads=[denb], writes=[rcb])
                    for i in range(2):
                        num, numb = cx.ps()
                        t.op(t.pe, lambda: mm_group(nc, num, [(Vm[:, mt, (2 * h + i) * 128:(2 * h + i + 1) * 128],
                                                               pms[mt][0][:]) for mt in range(2)]),
                             reads=[vmb, pms[0][1], pms[1][1]], writes=[numb])
                        t.op(t.dve, lambda: nc.vector.tensor_tensor(out=mo[i][0][:, tq * 512:(tq + 1) * 512],
                                                                    in0=num, in1=rc[:], op=ALU.mult),
                             reads=[numb, rcb], writes=[mo[i][1]])
                for i in range(2):
                    ch = 2 * h + i
                    t.dma(t.sp, memo_d[ch * 128:(ch + 1) * 128, :], mo[i][0][:], reads=[mo[i][1]])
            t.barrier()

        if stop <= 5:
            return nc
        with ExitStack() as es:
            br = []
            brb = [[Buf() for _ in range(4)] for _ in range(3)]
            brsrc = []
            for nm, dd in (("cact", cact_d), ("attn", attn_d), ("memo", memo_d)):
                br.append(sb(es, nc, nm, [128, 8, T], BF16))
                brsrc.append(dd.rearrange("(c p) t -> p c t", p=128))
            for tt_ in range(4):
                for b_ in range(3):
                    t.dma(t.sp, br[b_][:, :, tt_ * 512:(tt_ + 1) * 512], brsrc[b_][:, :, tt_ * 512:(tt_ + 1) * 512],
                          writes=[brb[b_][tt_]])
            wsrc = [w_conv_out.rearrange("(kc p) n -> p kc n", p=128),
                    w_attn_out.rearrange("(kc p) n -> p kc n", p=128),
                    w_mem_out.rearrange("(kc p) n -> p kc n", p=128)]
            wring = Ring(es, nc, "w5", [128, 3, 8, 512], BF16, 2)
            gring = Ring(es, nc, "g5", [128, 3, T], BF16, 2)
            tr = Ring(es, nc, "t5", [128, 3, 512], F32, 2)
            msr = Ring(es, nc, "ms", [128, T], BF16, 2)
            gv = gates_d.rearrange("(b m p) t -> p b m t", p=128, b=3)
            wl = {}

            def load5(gi):
                w, wb = wring.next()
                for b_ in range(3):
                    t.dma(t.pool, w[:, b_, :, :], wsrc[b_][:, :, gi * 512:(gi + 1) * 512], writes=[wb])
                wl[gi] = (w, wb)
            load5(0)
            for gi in range(4):
                if gi + 1 < 4:
                    load5(gi + 1)
                w, wb = wl.pop(gi)
                for mi in range(4):
                    m = gi * 4 + mi
                    gt, gtb = gring.next()
                    t.dma(t.sp, gt[:], gv[:, :, m, :], writes=[gtb])
                    ms, msb = msr.next()
                    for tt in range(4):
                        tm, tmb = tr.next()
                        for b_ in range(3):
                            ps, psb = cx.ps()
                            t.op(t.pe, lambda: mm_group(nc, ps, [(w[:, b_, kc, mi * 128:(mi + 1) * 128],
                                                                  br[b_][:, kc, tt * 512:(tt + 1) * 512]) for kc in range(8)]),
                                 reads=[wb, brb[b_][tt]], writes=[psb])
                            t.op(t.dve, lambda: nc.vector.tensor_tensor(out=tm[:, b_, :], in0=ps,
                                                                        in1=gt[:, b_, tt * 512:(tt + 1) * 512], op=ALU.mult),
                                 reads=[psb, gtb], writes=[tmb])
                        t.op(t.pool, lambda: nc.gpsimd.tensor_tensor(out=tm[:, 0, :], in0=tm[:, 0, :], in1=tm[:, 1, :], op=ALU.add),
                             reads=[tmb], writes=[tmb])
                        t.op(t.pool, lambda: nc.gpsimd.tensor_tensor(out=ms[:, tt * 512:(tt + 1) * 512], in0=tm[:, 0, :],
                                                                     in1=tm[:, 2, :], op=ALU.add),
                             reads=[tmb], writes=[msb])
                    t.dma(t.sp, merged_d[m * 128:(m + 1) * 128, :], ms[:], reads=[msb])
            t.barrier()

        if stop <= 6:
            return nc
        with ExitStack() as es:
            mg = sb(es, nc, "mg", [128, KC, T], BF16)
            mgb = [Buf() for _ in range(4)]
            mgsrc = merged_d.rearrange("(c p) t -> p c t", p=128)
            for tt_ in range(4):
                t.dma(t.sp, mg[:, :, tt_ * 512:(tt_ + 1) * 512], mgsrc[:, :, tt_ * 512:(tt_ + 1) * 512], writes=[mgb[tt_]])
            wov = w_o.rearrange("(kc p) n -> p kc n", p=128)
            wring = Ring(es, nc, "w6", [128, KC, 512], BF16, 2)
            hr = Ring(es, nc, "hr", [128, T], F32, 2)
            osr = Ring(es, nc, "os", [128, T], F32, 2)
            wl = {}

            def load6(gi):
                w, wb = wring.next()
                t.dma(t.pool, w[:], wov[:, :, gi * 512:(gi + 1) * 512], writes=[wb])
                wl[gi] = (w, wb)
            load6(0)
            for gi in range(4):
                if gi + 1 < 4:
                    load6(gi + 1)
                w, wb = wl.pop(gi)
                for mi in range(4):
                    m = gi * 4 + mi
                    hrt, hrb = hr.next()
                    t.dma(t.sp, hrt[:], hT[m * 128:(m + 1) * 128, HALO:TH], writes=[hrb])
                    os_, osb = osr.next()
                    for tt in range(4):
                        ps, psb = cx.ps()
                        t.op(t.pe, lambda: mm_group(nc, ps, [(w[:, kc, mi * 128:(mi + 1) * 128],
                                                              mg[:, kc, tt * 512:(tt + 1) * 512]) for kc in range(KC)]),
                             reads=[wb, mgb[tt]], writes=[psb])
                        t.op(t.dve, lambda: nc.vector.tensor_tensor(out=os_[:, tt * 512:(tt + 1) * 512], in0=ps,
                                                                    in1=hrt[:, tt * 512:(tt + 1) * 512], op=ALU.add),
                             reads=[psb, hrb], writes=[osb])
                    t.dma(t.sp, hmid[m * 128:(m + 1) * 128, :], os_[:], reads=[osb])
            t.barrier()
    return nc


F_G = 0
F_W0 = 16
F_W1 = 16 + 88
F_W2 = 16 + 176
F_B = 16 + 264
F_FING = 16 + 352
NV_FFN = 16 + 352 + 16
FT = 410


def build_ffn(final=False, dbg=False):
    nc = bass.Bass("TRN2", target_bir_lowering=False)
    cx = Ctx(nc)
    t = cx.t

    def din(name, shape):
        return nc.dram_tensor(name, shape, F32, kind="ExternalInput").ap()

    hm = din("hm", [D, T + 2])
    vecs_d = din("vecs", [128, NV_FFN])
    w_up = din("w_up", [D, 2 * D_FF])
    w_down = din("w_down", [D_FF, D])
    out = nc.dram_tensor("out", [D, T], F32, kind="ExternalOutput").ap()
    if dbg:
        act_d = nc.dram_tensor("act_d", [D_FF, T], BF16, kind="ExternalOutput").ap()
    else:
        act_d = nc.dram_tensor("act_d", [D_FF, T], BF16).ap()
    hout_d = nc.dram_tensor("hout_d", [D, T], F32).ap() if final else out

    wup_v = w_up.rearrange("(kc p) n -> p kc n", p=128)
    wdn_v = w_down.rearrange("(kc p) n -> p kc n", p=128)

    with ExitStack() as g:
        vecs = sb(g, nc, "vecs", [128, NV_FFN], F32)
        vb = Buf()
        t.dma(t.sp, vecs[:], vecs_d, writes=[vb])
        cn = load_consts(cx, g, None)
        eps_t, ones_bf = cn["eps"], cn["ones_bf"]
        t.barrier()

        with ExitStack() as es:
            NT = T + 2
            xn = sb(es, nc, "hn", [128, KC, NT], BF16)
            ntile = [(i * FT, FT) for i in range(5)]
            xnb = [Buf() for _ in ntile]
            with ExitStack() as es1:
                rmsnorm_phase(cx, es1, hm, ntile, vecs[:, F_G:F_G + 16], xn, xnb, eps_t, ones_bf)
                t.barrier()
            wring = Ring(es, nc, "wu", [128, 2, KC, 512], BF16, 2)
            tvr = Ring(es, nc, "tv", [128, FT], F32, 3)
            tgr = Ring(es, nc, "tg", [128, FT], F32, 3)
            str_ = Ring(es, nc, "sta", [128, T], BF16, 2)
            wl = {}

            def loadu(gi):
                w, wb = wring.next()
                t.dma(t.pool, w[:, 0, :, :], wup_v[:, :, gi * 512:(gi + 1) * 512], writes=[wb])
                t.dma(t.pool, w[:, 1, :, :], wup_v[:, :, D_FF + gi * 512: D_FF + (gi + 1) * 512], writes=[wb])
                wl[gi] = (w, wb)
            loadu(0)
            for gi in range(11):
                if gi + 1 < 11:
                    loadu(gi + 1)
                w, wb = wl.pop(gi)
                for mi in range(4):
                    c = gi * 4 + mi
                    st, stb = str_.next()
                    for tt in range(5):
                        o0 = tt * FT
                        no = min(FT, T - o0)
                        n = no + 2
                        res = []
                        for half, ring in ((0, tvr), (1, tgr)):
                            ps, psb = cx.ps()
                            t.op(t.pe, lambda: mm_group(nc, ps[:, :n], [(w[:, half, kc, mi * 128:(mi + 1) * 128],
                                                                         xn[:, kc, o0:o0 + n]) for kc in range(KC)]),
                                 reads=[wb, xnb[tt]], writes=[psb])
                            cc = half * NFC + c
                            tv, tvb = ring.next()
                            t.op(t.act, lambda: nc.scalar.activation(out=tv[:, :no], in_=ps[:, 2:2 + no], func=AF.Identity,
                                                                     bias=vecs[:, F_B + cc:F_B + cc + 1],
                                                                     scale=vecs[:, F_W2 + cc:F_W2 + cc + 1]),
                                 reads=[psb, vb], writes=[tvb])
                            t.op(t.dve, lambda: nc.vector.scalar_tensor_tensor(out=tv[:, :no], in0=ps[:, 1:1 + no],
                                                                               scalar=vecs[:, F_W1 + cc:F_W1 + cc + 1],
                                                                               in1=tv[:, :no], op0=ALU.mult, op1=ALU.add),
                                 reads=[psb, tvb, vb], writes=[tvb])
                            t.op(t.dve, lambda: nc.vector.scalar_tensor_tensor(out=tv[:, :no], in0=ps[:, 0:no],
                                                                               scalar=vecs[:, F_W0 + cc:F_W0 + cc + 1],
                                                                               in1=tv[:, :no], op0=ALU.mult, op1=ALU.add),
                                 reads=[psb, tvb, vb], writes=[tvb])
                            res.append((tv, tvb))
                        (tv, tvb), (tg, tgb) = res
                        t.op(t.act, lambda: nc.scalar.activation(out=tg[:, :no], in_=tg[:, :no], func=AF.Silu),
                             reads=[tgb], writes=[tgb])
                        t.op(t.dve, lambda: nc.vector.tensor_tensor(out=st[:, o0:o0 + no], in0=tg[:, :no], in1=tv[:, :no],
                                                                    op=ALU.mult),
                             reads=[tgb, tvb], writes=[stb])
                    t.dma(t.sp, act_d[c * 128:(c + 1) * 128, :], st[:], reads=[stb])
            t.barrier()

        with ExitStack() as es:
            wring = Ring(es, nc, "wd", [128, NFC, 256], BF16, 2)
            hr = Ring(es, nc, "hr", [128, 1024], F32, 2)
            osr = Ring(es, nc, "os", [128, 1024], F32, 2)
            av = act_d.rearrange("(c p) t -> p c t", p=128)
            for hf in range(2):
                with ExitStack() as es2:
                    at = sb(es2, nc, "at", [128, NFC, 1024], BF16)
                    atb = [Buf(), Buf()]
                    for tt_ in range(2):
                        for q2 in range(2):
                            t.dma(t.sp, at[:, q2 * 22:(q2 + 1) * 22, tt_ * 512:(tt_ + 1) * 512],
                                  av[:, q2 * 22:(q2 + 1) * 22, hf * 1024 + tt_ * 512: hf * 1024 + (tt_ + 1) * 512],
                                  writes=[atb[tt_]])
                    wl = {}

                    def loadd(gi):
                        w, wb = wring.next()
                        t.dma(t.pool, w[:], wdn_v[:, :, gi * 256:(gi + 1) * 256], writes=[wb])
                        wl[gi] = (w, wb)
                    loadd(0)
                    for gi in range(8):
                        if gi + 1 < 8:
                            loadd(gi + 1)
                        w, wb = wl.pop(gi)
                        for mi in range(2):
                            m = gi * 2 + mi
                            hrt, hrb = hr.next()
                            t.dma(t.sp, hrt[:], hm[m * 128:(m + 1) * 128, 2 + hf * 1024: 2 + (hf + 1) * 1024], writes=[hrb])
                            os_, osb = osr.next()
                            for tt in range(2):
                                ps, psb = cx.ps()
                                t.op(t.pe, lambda: mm_group(nc, ps, [(w[:, kc, mi * 128:(mi + 1) * 128],
                                                                      at[:, kc, tt * 512:(tt + 1) * 512]) for kc in range(NFC)]),
                                     reads=[wb, atb[tt]], writes=[psb])
                                t.op(t.dve, lambda: nc.vector.tensor_tensor(out=os_[:, tt * 512:(tt + 1) * 512], in0=ps,
                                                                            in1=hrt[:, tt * 512:(tt + 1) * 512], op=ALU.add),
                                     reads=[psb, hrb], writes=[osb])
                            t.dma(t.sp, hout_d[m * 128:(m + 1) * 128, hf * 1024:(hf + 1) * 1024], os_[:], reads=[osb])
                    t.barrier()

        if final:
            with ExitStack() as es:
                gv = vecs[:, F_FING:F_FING + 16]
                src = hout_d.rearrange("(kc p) t -> p kc t", p=128)
                dst = out.rearrange("(kc p) t -> p kc t", p=128)
                hring = Ring(es, nc, "fh", [128, KC, 512], F32, 2)
                sqring = Ring(es, nc, "fsq", [128, KC, 512], BF16, 2)
                sdring = Ring(es, nc, "fsd", [128, 512], F32, 2)
                rsring = Ring(es, nc, "frs", [128, 512], F32, 2)
                oring = Ring(es, nc, "fo", [128, KC, 512], F32, 2)
                for ti in range(4):
                    j0 = ti * 512
                    hb, hbb = hring.next()
                    t.dma(t.sp, hb[:], src[:, :, j0:j0 + 512], writes=[hbb])
                    sq, sqb = sqring.next()
                    t.op(t.act, lambda: nc.scalar.activation(out=sq[:], in_=hb[:], func=AF.Square), reads=[hbb], writes=[sqb])
                    ps, psb = cx.ps()
                    t.op(t.pe, lambda: mm_group(nc, ps, [(ones_bf[:, :], sq[:, kc, :]) for kc in range(KC)]),
                         reads=[sqb, cn["b"]], writes=[psb])
                    sd, sdb = sdring.next()
                    t.op(t.act, lambda: nc.scalar.activation(out=sd[:], in_=ps, func=AF.Sqrt, bias=eps_t[:, 0:1], scale=1.0 / D),
                         reads=[psb], writes=[sdb])
                    rs, rsb = rsring.next()
                    t.op(t.dve, lambda: nc.vector.reciprocal(rs[:], sd[:]), reads=[sdb], writes=[rsb])
                    o, ob = oring.next()

                    def body():
                        ins = None
                        for kc in range(KC):
                            ins = nc.vector.scalar_tensor_tensor(out=o[:, kc, :], in0=hb[:, kc, :], scalar=gv[:, kc:kc + 1],
                                                                 in1=rs[:], op0=ALU.mult, op1=ALU.mult)
                        return ins
                    t.op(t.dve, body, reads=[hbb, rsb, vb], writes=[ob])
                    t.dma(t.sp, dst[:, :, j0:j0 + 512], o[:], reads=[ob])
                t.barrier()
    return nc


def _cols(v, nch):
    return np.ascontiguousarray(np.asarray(v, np.float32).reshape(nch, 128).T)


def mixer_vecs(inp, l):
    v = np.zeros((128, NV_MIX), np.float32)
    v[:, V_MIXG:V_MIXG + 16] = _cols(inp["mix_norm_g"][l], 16)
    v[:, V_GATEB:V_GATEB + 48] = _cols(inp["gate_b"][l], 48)
    v[:, V_CONVB:V_CONVB + 8] = _cols(inp["conv_b"][l], 8)
    v[:, V_LNG:V_LNG + 8] = _cols(inp["conv_ln_g"][l], 8)
    v[:, V_LNB:V_LNB + 8] = _cols(inp["conv_ln_b"][l], 8)
    cw = np.asarray(inp["conv_w"][l], np.float32)
    cw = cw.reshape(31, 8, 128).transpose(2, 1, 0)
    v[:, V_CONVW:V_CONVW + 248] = cw.reshape(128, 248)
    v[:, V_MEMG:V_MEMG + 16] = _cols(inp["mem_norm_g"][l], 16)
    return v


def ffn_vecs(inp, l):
    v = np.zeros((128, NV_FFN), np.float32)
    v[:, F_G:F_G + 16] = _cols(inp["ffn_norm_g"][l], 16)
    fw = np.asarray(inp["ffn_conv_w"][l], np.float32)
    v[:, F_W0:F_W0 + 88] = _cols(fw[0], 88)
    v[:, F_W1:F_W1 + 88] = _cols(fw[1], 88)
    v[:, F_W2:F_W2 + 88] = _cols(fw[2], 88)
    v[:, F_B:F_B + 88] = _cols(inp["ffn_conv_b"][l], 88)
    v[:, F_FING:F_FING + 16] = _cols(inp["final_norm_g"], 16)
    return v


def bias_blocks(rel_bias_l):
    rb = np.asarray(rel_bias_l, np.float32)
    kl = np.arange(128)[:, None]
    ql = np.arange(128)[None, :]
    out = np.empty((16, 128, 640), np.float32)
    for dd in range(5):
        dist = 128 * (4 - dd) + ql - kl
        idx = np.clip(dist, -256, 256) + 256
        blk = rb[:, idx]
        if dd == 0:
            mask = (kl < 64) & (ql >= 64)
            blk = np.where(mask[None], np.float32(NEG), blk)
        if dd == 4:
            mask = (kl >= 64) & (ql < 64)
            blk = np.where(mask[None], np.float32(NEG), blk)
        out[:, :, dd * 128:(dd + 1) * 128] = blk
    return out


_progs = {}


def _prog(name):
    if name not in _progs:
        if name == "mixer":
            _progs[name] = build_mixer()
        elif name == "ffn":
            _progs[name] = build_ffn(final=False)
        else:
            _progs[name] = build_ffn(final=True)
    return _progs[name]


def _with_halo(hT_full, halo):
    outs = []
    B = hT_full.shape[0]
    for b in range(B):
        for s in range(4):
            t0 = s * T
            a = np.zeros((D, halo + T), np.float32)
            a[:, halo:] = hT_full[b, :, t0:t0 + T]
            if s > 0:
                a[:, :halo] = hT_full[b, :, t0 - halo:t0]
            outs.append(a)
    return outs


def kernel(**inp):
    inp = {k: np.asarray(v) for k, v in inp.items()}
    x = inp["x"].astype(np.float32, copy=False)
    B, S, _ = x.shape
    hT = np.ascontiguousarray(x.transpose(0, 2, 1))
    memT = np.ascontiguousarray(inp["mem"].transpose(0, 2, 1))
    ident = np.eye(128, dtype=np.float32)
    cores = list(range(NCORES))
    for l in range(DEPTH):
        hs = _with_halo(hT, HALO)
        mv = mixer_vecs(inp, l)
        bb = bias_blocks(inp["rel_bias"][l])
        in_maps = []
        for c in cores:
            b, s = divmod(c, 4)
            in_maps.append({
                "hT": hs[c], "memT": memT[b], "vecs": mv,
                "w_in": inp["w_in"][l], "w_conv_out": inp["w_conv_out"][l], "w_attn_out": inp["w_attn_out"][l],
                "w_mem_kv": inp["w_mem_kv"][l], "w_mem_out": inp["w_mem_out"][l], "w_o": inp["w_o"][l],
                "biasblk": bb,
                "halo_ones": np.full((128, 64), 0.0 if s == 0 else 1.0, np.float32),
                "ident": ident,
            })
        res = run_bass_kernel_spmd(_prog("mixer"), in_maps, core_ids=cores)
        hmT = np.stack([np.concatenate([res.results[b * 4 + s]["hmid"] for s in range(4)], axis=1) for b in range(B)])
        hs2 = _with_halo(hmT, 2)
        fv = ffn_vecs(inp, l)
        in_maps = [{"hm": hs2[c], "vecs": fv, "w_up": inp["w_up"][l], "w_down": inp["w_down"][l]} for c in cores]
        res = run_bass_kernel_spmd(_prog("ffn_final" if l == DEPTH - 1 else "ffn"), in_maps, core_ids=cores)
        hT = np.stack([np.concatenate([res.results[b * 4 + s]["out"] for s in range(4)], axis=1) for b in range(B)])
    return np.ascontiguousarray(hT.transpose(0, 2, 1)).astype(np.float32)
```
